# Optimizing a Trainium2 kernel written in Bass

```python
import jax, jax.numpy as jnp
from jax import lax
import numpy as np

D_MODEL = 2048
BATCH = 2
SEQ = 16384
DEPTH = 1

N_MEM = 256
LRU_WIDTH = D_MODEL
LRU_BLOCKS = 16
LRU_BLOCK_DIM = LRU_WIDTH // LRU_BLOCKS
CONV_WIDTH = 4
LRU_C = 8.0
HEAD_DIM = 128
N_Q_HEADS = 16
N_KV_HEADS = 4
Q_GROUP = N_Q_HEADS // N_KV_HEADS
WINDOW = 128
BLOCK = 128
N_X_HEADS = 4
X_HEAD_DIM = D_MODEL // N_X_HEADS
D_FF = 4 * D_MODEL
N_BRANCH = 3
ROPE_THETA = 10000.0
EPS = 1e-6
SPLITS = (LRU_WIDTH, LRU_WIDTH, N_Q_HEADS * HEAD_DIM, N_KV_HEADS * HEAD_DIM,
          N_KV_HEADS * HEAD_DIM, N_X_HEADS * X_HEAD_DIM, N_BRANCH * D_MODEL)

kernel_name = 'hybrid_rglru_swa_memxattn_block'


def rmsnorm(x, g):
    xf = x.astype(jnp.float32)
    var = jnp.mean(xf * xf, axis=-1, keepdims=True)
    return (xf * lax.rsqrt(var + EPS) * g.astype(jnp.float32)).astype(x.dtype)


def rope(t, positions):
    half = t.shape[-1] // 2
    freqs = ROPE_THETA ** (-jnp.arange(half, dtype=jnp.float32) / half)
    ang = positions.astype(jnp.float32)[..., None] * freqs
    cos = jnp.cos(ang)[:, :, None, :]
    sin = jnp.sin(ang)[:, :, None, :]
    tf = t.astype(jnp.float32)
    t1, t2 = tf[..., :half], tf[..., half:]
    return jnp.concatenate([t1 * cos - t2 * sin, t2 * cos + t1 * sin], axis=-1).astype(t.dtype)


def _lin_combine(c1, c2):
    a1, b1 = c1
    a2, b2 = c2
    return a1 * a2, a2 * b1 + b2


def rglru_direction(xc, w_r, b_r, w_i, b_i, lam, reverse):
    B, S, C = xc.shape
    xb = xc.reshape(B, S, LRU_BLOCKS, LRU_BLOCK_DIM)
    r = jax.nn.sigmoid((jnp.einsum('bshi,hij->bshj', xb, w_r) + b_r).reshape(B, S, C).astype(jnp.float32))
    i = jax.nn.sigmoid((jnp.einsum('bshi,hij->bshj', xb, w_i) + b_i).reshape(B, S, C).astype(jnp.float32))
    log_a = -LRU_C * r * jax.nn.softplus(-lam.astype(jnp.float32))
    a = jnp.exp(log_a)
    u = jnp.sqrt(-jnp.expm1(2.0 * log_a)) * (i * xc.astype(jnp.float32))
    if reverse:
        a, u = jnp.flip(a, axis=1), jnp.flip(u, axis=1)
    _, h = lax.associative_scan(_lin_combine, (a, u), axis=1)
    if reverse:
        h = jnp.flip(h, axis=1)
    return h


def local_attention(q, k, v, sink):
    B, S, _, D = q.shape
    nb = S // BLOCK
    qb = q.reshape(B, nb, BLOCK, N_KV_HEADS, Q_GROUP, D)

    def band(t):
        tp = jnp.pad(t, ((0, 0), (BLOCK, BLOCK), (0, 0), (0, 0))).reshape(B, nb + 2, BLOCK, N_KV_HEADS, D)
        return jnp.concatenate([tp[:, :-2], tp[:, 1:-1], tp[:, 2:]], axis=2)

    kb, vb = band(k), band(v)
    s = jnp.einsum('bnqkgd,bnskd->bnkgqs', qb, kb).astype(jnp.float32) * (D ** -0.5)
    blk = jnp.arange(nb, dtype=jnp.int32)[:, None] * BLOCK
    qpos = blk + jnp.arange(BLOCK, dtype=jnp.int32)[None, :]
    kpos = blk - BLOCK + jnp.arange(3 * BLOCK, dtype=jnp.int32)[None, :]
    rel = kpos[:, None, :] - qpos[:, :, None]
    valid = (jnp.abs(rel) <= WINDOW) & (kpos[:, None, :] >= 0) & (kpos[:, None, :] < S)
    s = jnp.where(valid[None, :, None, None], s, -jnp.inf)
    sink_f = sink.astype(jnp.float32).reshape(N_KV_HEADS, Q_GROUP)[None, None, :, :, None, None]
    m = jnp.maximum(jnp.max(s, axis=-1, keepdims=True), sink_f)
    p = jnp.exp(s - m)
    denom = jnp.sum(p, axis=-1, keepdims=True) + jnp.exp(sink_f - m)
    o = jnp.einsum('bnkgqs,bnskd->bnqkgd', (p / denom).astype(v.dtype), vb)
    return o.reshape(B, S, N_Q_HEADS * D)


def memory_attention(q, mk, mv):
    B, S = q.shape[0], q.shape[1]
    s = jnp.einsum('bshd,bmhd->bhsm', q, mk).astype(jnp.float32) * (X_HEAD_DIM ** -0.5)
    p = jax.nn.softmax(s, axis=-1).astype(mv.dtype)
    return jnp.einsum('bhsm,bmhd->bshd', p, mv).reshape(B, S, N_X_HEADS * X_HEAD_DIM)


def hybrid_layer(x, mem, positions, norm_mix_pre, norm_mix_post, norm_mem, w_in, b_gate,
                 conv_w, conv_b, wr_f, br_f, wi_f, bi_f, lam_f, wr_b, br_b, wi_b, bi_b, lam_b,
                 attn_sink, w_mem_kv, w_br_lru, w_br_attn, w_br_mem, w_out,
                 norm_mlp_pre, norm_mlp_post, w_up, w_down):
    B, S, _ = x.shape
    h = rmsnorm(x, norm_mix_pre)
    proj = h @ w_in
    idx = [int(c) for c in np.cumsum(SPLITS)[:-1]]
    xr, gr, q, k, v, qm, gl = jnp.split(proj, idx, axis=-1)

    xc = lax.conv_general_dilated(xr, conv_w[:, None, :], window_strides=(1,),
                                  padding=[(1, CONV_WIDTH - 2)],
                                  dimension_numbers=('NWC', 'WIO', 'NWC'),
                                  feature_group_count=LRU_WIDTH) + conv_b
    h_lru = (rglru_direction(xc, wr_f, br_f, wi_f, bi_f, lam_f, False)
             + rglru_direction(xc, wr_b, br_b, wi_b, bi_b, lam_b, True))
    y_lru = (h_lru * jax.nn.gelu(gr.astype(jnp.float32))).astype(x.dtype)

    q = rope(q.reshape(B, S, N_Q_HEADS, HEAD_DIM), positions)
    k = rope(k.reshape(B, S, N_KV_HEADS, HEAD_DIM), positions)
    v = v.reshape(B, S, N_KV_HEADS, HEAD_DIM)
    y_attn = local_attention(q, k, v, attn_sink)

    mn = rmsnorm(mem, norm_mem)
    mk, mv = jnp.split(mn @ w_mem_kv, 2, axis=-1)
    M = mem.shape[1]
    y_mem = memory_attention(qm.reshape(B, S, N_X_HEADS, X_HEAD_DIM),
                             mk.reshape(B, M, N_X_HEADS, X_HEAD_DIM),
                             mv.reshape(B, M, N_X_HEADS, X_HEAD_DIM))

    gates = jax.nn.sigmoid((gl + b_gate).astype(jnp.float32)).reshape(B, S, N_BRANCH, D_MODEL)
    merged = (gates[:, :, 0] * (y_lru @ w_br_lru).astype(jnp.float32)
              + gates[:, :, 1] * (y_attn @ w_br_attn).astype(jnp.float32)
              + gates[:, :, 2] * (y_mem @ w_br_mem).astype(jnp.float32)).astype(x.dtype)
    x = x + rmsnorm(merged @ w_out, norm_mix_post)

    hm = rmsnorm(x, norm_mlp_pre)
    u = jnp.square(jax.nn.relu(hm @ w_up))
    x = x + rmsnorm(u @ w_down, norm_mlp_post)
    return x


def setup_inputs(seed: int = 0) -> dict:
    key = jax.random.key(seed)
    ks = jax.random.split(key, 40)
    L = DEPTH
    f32 = jnp.float32

    def nrm(k, shape, scale):
        return jax.random.normal(k, shape, f32) * scale

    def gain(k, shape):
        return 1.0 + 0.05 * jax.random.normal(k, shape, f32)

    def lam_init(k):
        a0 = jax.random.uniform(k, (L, LRU_WIDTH), f32, minval=0.9, maxval=0.999)
        return jnp.log(a0) - jnp.log1p(-a0)

    n_in = sum(SPLITS)
    bd = LRU_BLOCK_DIM
    return {
        'x': nrm(ks[0], (BATCH, SEQ, D_MODEL), 1.0),
        'mem': nrm(ks[1], (BATCH, N_MEM, D_MODEL), 1.0),
        'positions': (jnp.arange(SEQ, dtype=jnp.int32)[None, :]
                      + jax.random.randint(ks[2], (BATCH, 1), 0, 1024, dtype=jnp.int32)),
        'norm_mix_pre': gain(ks[3], (L, D_MODEL)),
        'norm_mix_post': gain(ks[4], (L, D_MODEL)),
        'norm_mem': gain(ks[5], (L, D_MODEL)),
        'w_in': nrm(ks[6], (L, D_MODEL, n_in), D_MODEL ** -0.5),
        'b_gate': nrm(ks[7], (L, N_BRANCH * D_MODEL), 0.1),
        'conv_w': nrm(ks[8], (L, CONV_WIDTH, LRU_WIDTH), CONV_WIDTH ** -0.5),
        'conv_b': nrm(ks[9], (L, LRU_WIDTH), 0.02),
        'wr_f': nrm(ks[10], (L, LRU_BLOCKS, bd, bd), bd ** -0.5),
        'br_f': nrm(ks[11], (L, LRU_BLOCKS, bd), 0.02),
        'wi_f': nrm(ks[12], (L, LRU_BLOCKS, bd, bd), bd ** -0.5),
        'bi_f': nrm(ks[13], (L, LRU_BLOCKS, bd), 0.02),
        'lam_f': lam_init(ks[14]),
        'wr_b': nrm(ks[15], (L, LRU_BLOCKS, bd, bd), bd ** -0.5),
        'br_b': nrm(ks[16], (L, LRU_BLOCKS, bd), 0.02),
        'wi_b': nrm(ks[17], (L, LRU_BLOCKS, bd, bd), bd ** -0.5),
        'bi_b': nrm(ks[18], (L, LRU_BLOCKS, bd), 0.02),
        'lam_b': lam_init(ks[19]),
        'attn_sink': nrm(ks[20], (L, N_Q_HEADS), 0.5),
        'w_mem_kv': nrm(ks[21], (L, D_MODEL, 2 * N_X_HEADS * X_HEAD_DIM), D_MODEL ** -0.5),
        'w_br_lru': nrm(ks[22], (L, LRU_WIDTH, D_MODEL), LRU_WIDTH ** -0.5),
        'w_br_attn': nrm(ks[23], (L, N_Q_HEADS * HEAD_DIM, D_MODEL), (N_Q_HEADS * HEAD_DIM) ** -0.5),
        'w_br_mem': nrm(ks[24], (L, N_X_HEADS * X_HEAD_DIM, D_MODEL), (N_X_HEADS * X_HEAD_DIM) ** -0.5),
        'w_out': nrm(ks[25], (L, D_MODEL, D_MODEL), D_MODEL ** -0.5),
        'norm_mlp_pre': gain(ks[26], (L, D_MODEL)),
        'norm_mlp_post': gain(ks[27], (L, D_MODEL)),
        'w_up': nrm(ks[28], (L, D_MODEL, D_FF), D_MODEL ** -0.5),
        'w_down': nrm(ks[29], (L, D_FF, D_MODEL), D_FF ** -0.5),
    }


def reference(x, mem, positions, norm_mix_pre, norm_mix_post, norm_mem, w_in, b_gate,
              conv_w, conv_b, wr_f, br_f, wi_f, bi_f, lam_f, wr_b, br_b, wi_b, bi_b, lam_b,
              attn_sink, w_mem_kv, w_br_lru, w_br_attn, w_br_mem, w_out,
              norm_mlp_pre, norm_mlp_post, w_up, w_down):
    for l in range(DEPTH):
        x = hybrid_layer(x, mem, positions, norm_mix_pre[l], norm_mix_post[l], norm_mem[l],
                         w_in[l], b_gate[l], conv_w[l], conv_b[l],
                         wr_f[l], br_f[l], wi_f[l], bi_f[l], lam_f[l],
                         wr_b[l], br_b[l], wi_b[l], bi_b[l], lam_b[l],
                         attn_sink[l], w_mem_kv[l], w_br_lru[l], w_br_attn[l], w_br_mem[l],
                         w_out[l], norm_mlp_pre[l], norm_mlp_post[l], w_up[l], w_down[l])
    return x
```

```python
import os
import numpy as np
from contextlib import ExitStack
import concourse.bass as bass
import concourse.mybir as mybir
from concourse.bass_utils import run_bass_kernel_spmd

F32 = mybir.dt.float32
BF16 = mybir.dt.bfloat16
I32 = mybir.dt.int32
ALU = mybir.AluOpType
AF = mybir.ActivationFunctionType

D = 2048
NCH = 16
T = 256
TB = T // 128
WIN = T + 256
NBLK = WIN // 128
DFF = 8192
NIN = 15360
EPS = 1e-6
NSLAB = 2


class Sem:
    def __init__(self, nc, name, reg):
        self.h = nc.alloc_semaphore(name)
        self.v = 0
        reg.append(self)


class Buf:
    __slots__ = ("name", "w", "r")

    def __init__(self, name):
        self.name = name
        self.w = None
        self.r = {}


class Ctx:
    def __init__(self, nc):
        self.nc = nc
        self.sems = []
        self.eng = {"pe": nc.tensor, "act": nc.scalar, "dve": nc.vector, "pool": nc.gpsimd, "sp": nc.sync}
        self.esem = {k: Sem(nc, "s_" + k, self.sems) for k in ("pe", "act", "dve", "pool")}
        self.seen = {k: {} for k in self.eng}

    def sem(self, name):
        return Sem(self.nc, name, self.sems)

    def _deps(self, reads, writes):
        d = {}

        def add(s, v):
            if d.get(s, 0) < v:
                d[s] = v
        for b in reads:
            if b.w is not None:
                add(*b.w)
        for b in writes:
            if b.w is not None:
                add(*b.w)
            for s, v in b.r.items():
                add(s, v)
        return d

    def _wait(self, e, d):
        seen = self.seen[e]
        for s, v in d.items():
            if seen.get(s, 0) < v:
                self.eng[e].wait_ge(s.h, v)
                seen[s] = v

    def _commit(self, ev, reads, writes):
        s, v = ev
        for b in writes:
            b.w = ev
            b.r = {}
        for b in reads:
            if b.r.get(s, 0) < v:
                b.r[s] = v

    def op(self, e, fn, reads=(), writes=()):
        self._wait(e, self._deps(reads, writes))
        ins = fn()
        s = self.esem[e]
        s.v += 1
        ins.then_inc(s.h, 1)
        self._commit((s, s.v), reads, writes)

    def dma(self, e, sem, out, in_, reads=(), writes=()):
        self._wait(e, self._deps(reads, writes))
        ins = self.eng[e].dma_start(out=out, in_=in_)
        sem.v += 16
        ins.then_inc(sem.h, 16)
        self._commit((sem, sem.v), reads, writes)

    def barrier(self):
        for e in self.eng:
            d = {s: s.v for s in self.sems if s.v > 0}
            self._wait(e, d)


def build(NT):
    L = NT * T
    LW = L + 256
    nc = bass.Bass("TRN2", target_bir_lowering=False)
    K = Ctx(nc)

    def din(name, shape, dt=F32):
        return nc.dram_tensor(name, shape, dt, kind="ExternalInput").ap()

    def dint(name, shape, dt=F32):
        return nc.dram_tensor(name, shape, dt, kind="Internal").ap()

    xw = din("xw", [LW, D])
    xo = din("xo", [3, LW, D])
    posw = din("posw", [1, LW], I32)
    memb = din("memb", [256, D])
    w_in = din("w_in", [D, NIN])
    w_mem_kv = din("w_mem_kv", [D, 2 * D])
    w_br = [din("w_br_lru", [D, D]), din("w_br_attn", [D, D]), din("w_br_mem", [D, D])]
    w_out = din("w_out", [D, D])
    w_up = din("w_up", [D, DFF])
    w_down = din("w_down", [DFF, D])
    gw_d = din("gw", [5, 2, 128, NCH, 128])
    gv_d = din("gv", [128, 5, 3, NCH])
    ctap_d = din("ctap", [128, 4, NCH, 5])
    cbias_d = din("cbias", [128, NCH])
    sel_d = din("sel", [128, 3, 2])
    nmp_d = din("nmp", [128, NCH])
    nmem_d = din("nmem", [128, NCH])
    bgate_d = din("bgate", [128, 48])
    rows_d = din("rows", [3, D])
    sink_d = din("sinkb", [128, NCH])
    freq_d = din("freq", [128, 2])
    masks_d = din("masks", [128, 4, 4, 128])
    ident_d = din("ident", [128, 128])
    out_d = nc.dram_tensor("out", [L, D], F32, kind="ExternalOutput").ap()

    DBG = bool(int(os.environ.get("KDBG", "0")))
    if DBG:
        dbg_x1 = nc.dram_tensor("dbg_x1", [L, D], F32, kind="ExternalOutput").ap()
        dbg_mo = nc.dram_tensor("dbg_mo", [L, D], F32, kind="ExternalOutput").ap()
        dbg_hf = nc.dram_tensor("dbg_hf", [NT, 128, NCH, T], F32, kind="ExternalOutput").ap()
        dbg_y = nc.dram_tensor("dbg_y", [128, 64, T], BF16, kind="ExternalOutput").ap()
        dbg_h = nc.dram_tensor("dbg_h", [128, NCH, WIN], BF16, kind="ExternalOutput").ap()
    hf_s = dint("hf_s", [NT, 128, NCH, T])
    ab_s = dint("ab_s", [NT, 128, NCH, T])
    ub_s = dint("ub_s", [NT, 128, NCH, T])
    mo_s = dint("mo_s", [L, D])
    x1_s = dint("x1_s", [L, D])
    hf_b = [[Buf("hf%d_%d" % (j, g)) for g in range(4)] for j in range(NT)]
    ab_b = [[Buf("ab%d_%d" % (j, g)) for g in range(4)] for j in range(NT)]
    ub_b = [[Buf("ub%d_%d" % (j, g)) for g in range(4)] for j in range(NT)]
    mo_b = [[Buf("mo%d_%d" % (j, t)) for t in range(TB)] for j in range(NT)]
    x1_b = [[Buf("x1%d_%d" % (j, t)) for t in range(TB)] for j in range(NT)]

    top = ExitStack()

    def sbt(es, name, shape, dt=F32):
        return es.enter_context(nc.sbuf_tensor("sb_" + name, shape, dt))

    NPS = 5
    ps_t = [top.enter_context(nc.psum_tensor("ps%d" % i, [128, 512], F32)) for i in range(NPS)]
    ps_b = [Buf("ps%d" % i) for i in range(NPS)]
    pst_t = [top.enter_context(nc.psum_tensor("pst%d" % i, [128, 1024], BF16)) for i in range(2)]
    pst_b = [Buf("pst%d" % i) for i in range(2)]
    psh_t = top.enter_context(nc.psum_tensor("psh", [128, 512], F32))
    psh_b = Buf("psh")
    ps_rr = [0]

    def next_ps():
        i = ps_rr[0] % NPS
        ps_rr[0] += 1
        return ps_t[i], ps_b[i]

    def mm_group(out_ap, pairs, reads, writes):
        def fn():
            n = len(pairs)
            ins = None
            for i, (l, r) in enumerate(pairs):
                ins = nc.tensor.matmul(out_ap, lhsT=l, rhs=r, start=(i == 0), stop=(i == n - 1))
            return ins
        K.op("pe", fn, reads, writes)

    csem = K.sem("csem")
    cb = Buf("consts")
    ident_f = sbt(top, "ident_f", [128, 128])
    ident_b = sbt(top, "ident_b", [128, 128], BF16)
    ones_b = sbt(top, "ones_b", [128, 128], BF16)
    masks_f = sbt(top, "masks_f", [128, 4, 4, 128])
    masks_b = sbt(top, "masks_b", [128, 4, 4, 128], BF16)
    gv = sbt(top, "gv", [128, 5, 3, NCH])
    gvh = sbt(top, "gvh", [128, 5, 3, NCH])
    cl = sbt(top, "cl", [128, 5, NCH])
    clh = sbt(top, "clh", [128, 5, NCH])
    clq = sbt(top, "clq", [128, 5, NCH])
    ctmp = sbt(top, "ctmp", [128, 5, NCH])
    ctmp2 = sbt(top, "ctmp2", [128, 5, NCH])
    ctap = sbt(top, "ctap", [128, 4, NCH, 5])
    cbias = sbt(top, "cbias", [128, NCH])
    sel = sbt(top, "sel", [128, 3, 2])
    nmp = sbt(top, "nmp", [128, NCH])
    nmem = sbt(top, "nmem", [128, NCH])
    bgate = sbt(top, "bgate", [128, 48])
    esink = sbt(top, "esink", [128, NCH])
    freq = sbt(top, "freq", [128, 2])
    neghalf = sbt(top, "neghalf", [128, 8])
    mkT = sbt(top, "mkT", [128, NCH, 256], BF16)
    mv = sbt(top, "mv", [128, 2, D], BF16)
    carry = sbt(top, "carry", [128, 2, NCH])
    mk_b = Buf("mkT")
    mv_b = Buf("mv")
    carry_b = Buf("carry")

    for dst, src in ((ident_f, ident_d), (masks_f, masks_d), (gv, gv_d), (ctap, ctap_d), (cbias, cbias_d),
                     (sel, sel_d), (nmp, nmp_d), (nmem, nmem_d), (bgate, bgate_d), (esink, sink_d), (freq, freq_d)):
        K.dma("sp", csem, dst[:], src, writes=[cb])
    V = nc.vector
    A = nc.scalar
    G = nc.gpsimd
    K.op("dve", lambda: V.tensor_copy(out=ident_b[:], in_=ident_f[:]), [cb], [cb])
    K.op("dve", lambda: V.tensor_copy(out=masks_b[:], in_=masks_f[:]), [cb], [cb])
    K.op("dve", lambda: V.memset(ones_b[:], 1.0), [], [cb])
    K.op("dve", lambda: V.memset(neghalf[:], -0.5), [], [cb])
    K.op("dve", lambda: V.memset(carry[:], 0.0), [], [carry_b])
    K.op("dve", lambda: V.tensor_scalar(out=gvh[:], in0=gv[:], scalar1=0.5, scalar2=None, op0=ALU.mult), [cb], [cb])
    lam = gv[:, :, 2, :]
    K.op("act", lambda: A.activation(out=ctmp[:], in_=lam, func=AF.Abs), [cb], [cb])
    K.op("act", lambda: A.activation(out=ctmp[:], in_=ctmp[:], func=AF.Exp, scale=-1.0), [cb], [cb])
    K.op("act", lambda: A.activation(out=ctmp[:], in_=ctmp[:], func=AF.Ln, bias=1.0), [cb], [cb])
    K.op("dve", lambda: V.tensor_scalar(out=ctmp2[:], in0=lam, scalar1=-1.0, scalar2=0.0, op0=ALU.mult, op1=ALU.max), [cb], [cb])
    K.op("dve", lambda: V.tensor_tensor(out=ctmp[:], in0=ctmp[:], in1=ctmp2[:], op=ALU.add), [cb], [cb])
    K.op("dve", lambda: V.tensor_scalar(out=cl[:], in0=ctmp[:], scalar1=-8.0, scalar2=None, op0=ALU.mult), [cb], [cb])
    K.op("dve", lambda: V.tensor_scalar(out=clh[:], in0=ctmp[:], scalar1=-4.0, scalar2=None, op0=ALU.mult), [cb], [cb])
    K.op("dve", lambda: V.tensor_scalar(out=clq[:], in0=ctmp[:], scalar1=-2.0, scalar2=None, op0=ALU.mult), [cb], [cb])
    K.op("act", lambda: A.activation(out=esink[:], in_=esink[:], func=AF.Exp), [cb], [cb])

    slab_t = [sbt(top, "slab%d" % i, [128, NCH, 512], BF16) for i in range(NSLAB)]
    slab_b = [Buf("slab%d" % i) for i in range(NSLAB)]
    slab_s = [K.sem("slabsem%d" % i) for i in range(NSLAB)]
    slab_rr = [0]

    def load_slab(Wap, r0, c0, ncols=512):
        i = slab_rr[0] % NSLAB
        slab_rr[0] += 1
        src = Wap[r0:r0 + D, c0:c0 + ncols].rearrange("(c p) j -> p c j", p=128)
        K.dma("pool", slab_s[i], slab_t[i][:, :, 0:ncols], src, writes=[slab_b[i]])
        return slab_t[i], slab_b[i]

    def run_slabs(specs, compute, pre=NSLAB - 1):
        pend = []
        for idx, sp in enumerate(specs):
            pend.append((idx, sp, load_slab(*sp)))
            if len(pend) > pre:
                i0, s0, (t0, b0) = pend.pop(0)
                compute(i0, t0, b0)
        for i0, s0, (t0, b0) in pend:
            compute(i0, t0, b0)

    tok_t = [sbt(top, "tok%d" % i, [128, D]) for i in range(2)]
    tok_b = [Buf("tok%d" % i) for i in range(2)]
    tok_s = [K.sem("toksem%d" % i) for i in range(2)]
    xs_t = [sbt(top, "xs%d" % i, [128, D], BF16) for i in range(2)]
    xs_b = [Buf("xs%d" % i) for i in range(2)]
    st_t = sbt(top, "stats", [128, 8])
    st_b = Buf("stats")
    tok_rr = [0]

    def rstd_from(src_t, src_b, junk_t, junk_b):
        K.op("act", lambda: A.activation(out=junk_t[:], in_=src_t[:], func=AF.Square, accum_out=st_t[:, 0:1]),
             [src_b], [junk_b, st_b])
        K.op("dve", lambda: V.tensor_scalar(out=st_t[:, 1:2], in0=st_t[:, 0:1], scalar1=1.0 / D, scalar2=EPS,
                                            op0=ALU.mult, op1=ALU.add), [st_b], [st_b])
        K.op("act", lambda: A.activation(out=st_t[:, 3:4], in_=st_t[:, 1:2], func=AF.Sqrt), [st_b], [st_b])
        K.op("dve", lambda: V.reciprocal(out=st_t[:, 2:3], in_=st_t[:, 3:4]), [st_b], [st_b])

    def to_featmajor(xsrc_t, xsrc_b, dst_t, dst_b, col0, gam, nrows=128):
        for half in range(2):
            pt, pb = pst_t[half], pst_b[half]

            def fn(half=half, pt=pt):
                ins = None
                for c8 in range(8):
                    c = half * 8 + c8
                    ins = nc.tensor.transpose(out=pt[:, c8 * 128:(c8 + 1) * 128], in_=xsrc_t[:, c * 128:(c + 1) * 128],
                                              identity=ident_b[:])
                return ins
            K.op("pe", fn, [xsrc_b, cb], [pb])
            if gam is None:
                K.op("act", lambda half=half, pt=pt: A.copy(
                    out=dst_t[:, half * 8:(half + 1) * 8, col0:col0 + 128],
                    in_=pt[:].rearrange("p (c t) -> p c t", c=8)), [pb], [dst_b])
            else:
                for c8 in range(8):
                    c = half * 8 + c8
                    e = "dve"
                    if e == "act":
                        K.op("act", lambda c=c, c8=c8, pt=pt: A.activation(
                            out=dst_t[:, c, col0:col0 + 128], in_=pt[:, c8 * 128:(c8 + 1) * 128],
                            func=AF.Copy, scale=gam[:, c:c + 1]), [pb, cb], [dst_b])
                    else:
                        K.op("dve", lambda c=c, c8=c8, pt=pt: V.tensor_scalar(
                            out=dst_t[:, c, col0:col0 + 128], in0=pt[:, c8 * 128:(c8 + 1) * 128],
                            scalar1=gam[:, c:c + 1], scalar2=None, op0=ALU.mult), [pb, cb], [dst_b])

    def make_h(src_rows, nblk, dst_t, dst_b, gam):
        for blk in range(nblk):
            i = tok_rr[0] % 2
            tok_rr[0] += 1
            K.dma("sp", tok_s[i], tok_t[i][:], src_rows[blk * 128:(blk + 1) * 128, :], writes=[tok_b[i]])
            rstd_from(tok_t[i], tok_b[i], xs_t[i], xs_b[i])
            K.op("dve", lambda i=i: V.tensor_scalar(out=xs_t[i][:], in0=tok_t[i][:], scalar1=st_t[:, 2:3], scalar2=None,
                                                    op0=ALU.mult), [tok_b[i], st_b], [xs_b[i]])
            to_featmajor(xs_t[i], xs_b[i], dst_t, dst_b, blk * 128, gam)

    with ExitStack() as es:
        mnT = sbt(es, "mnT", [128, NCH, 256], BF16)
        mn_b = Buf("mnT")
        make_h(memb, 2, mnT, mn_b, nmem)

        def comp_mk(idx, st, sbf):
            for o4 in range(4):
                c = idx * 4 + o4
                pt, pb = next_ps()
                mm_group(pt[:, 0:256], [(st[:, k, o4 * 128:(o4 + 1) * 128], mnT[:, k, :]) for k in range(NCH)],
                         [sbf, mn_b], [pb])
                K.op("act", lambda c=c, pt=pt: A.copy(out=mkT[:, c, :], in_=pt[:, 0:256]), [pb], [mk_b])
        run_slabs([(w_mem_kv, 0, g * 512) for g in range(4)], comp_mk)

        def comp_mv(idx, st, sbf):
            for blk in range(2):
                pt, pb = next_ps()
                mm_group(pt[:, :], [(mnT[:, k, blk * 128:(blk + 1) * 128], st[:, k, :]) for k in range(NCH)],
                         [sbf, mn_b], [pb])
                K.op("act", lambda blk=blk, pt=pt, idx=idx: A.copy(out=mv[:, blk, idx * 512:(idx + 1) * 512], in_=pt[:, :]),
                     [pb], [mv_b])
        run_slabs([(w_mem_kv, 0, D + g * 512) for g in range(4)], comp_mv)
        K.barrier()

    with ExitStack() as es:
        hL = sbt(es, "hL", [128, NCH, WIN], BF16)
        hL_b = Buf("hL")
        gwt = [[sbt(es, "gw%d_%d" % (d, ri), [128, NCH, 128], BF16) for ri in range(2)] for d in range(2)]
        gw_b = Buf("gw")
        gwsem = K.sem("gwsem")
        xr_sb = sbt(es, "xr_sb", [128, 4, T + 4])
        xr_b = Buf("xr_sb")
        xc = sbt(es, "xc", [128, 4, T])
        xc_b = Buf("xc")
        xcb = sbt(es, "xcb", [128, 4, T], BF16)
        xcb_b = Buf("xcb")
        Rt = [sbt(es, "R%d" % d, [128, 4, T]) for d in range(2)]
        It = [sbt(es, "I%d" % d, [128, 4, T]) for d in range(2)]
        At = [sbt(es, "A%d" % d, [128, 4, T]) for d in range(2)]
        R_b = [Buf("R%d" % d) for d in range(2)]
        I_b = [Buf("I%d" % d) for d in range(2)]
        A_b = [Buf("A%d" % d) for d in range(2)]
        Wt = sbt(es, "Wt", [128, 4, T])
        W_b = Buf("Wt")
        HF = sbt(es, "HF", [128, 4, T])
        HF_b = Buf("HF")
        spsem = K.sem("spsem")
        st8 = sbt(es, "st8", [128, NCH])
        st8_b = Buf("st8")
        rsum = sbt(es, "rsum", [128, NCH])
        rs_t = sbt(es, "rs_t", [128, NCH])
        rs_b = Buf("rsum")
        aend = sbt(es, "aend", [128, NCH])

        def load_gw(slots):
            for d, sl in enumerate(slots):
                for ri in range(2):
                    K.dma("pool", gwsem, gwt[d][ri][:], gw_d[sl, ri], writes=[gw_b])

        def lru_tile(src_rows, slots, tapi, spill_j):
            nd = len(slots)
            make_h(src_rows, NBLK, hL, hL_b, nmp)

            def comp(og, st, sbf):
                psB, psB_b = psh_t, psh_b
                for o4 in range(4):
                    c = og * 4 + o4
                    pA, pA_b = next_ps()
                    lw = [st[:, k, o4 * 128:(o4 + 1) * 128] for k in range(NCH)]
                    mm_group(pA[:, 0:T], [(lw[k], hL[:, k, 126:126 + T]) for k in range(NCH)], [sbf, hL_b], [pA_b])
                    mm_group(psB[:, o4 * 4:o4 * 4 + 4], [(lw[k], hL[:, k, 126 + T:130 + T]) for k in range(NCH)], [sbf, hL_b], [psB_b])
                    K.op("act", lambda o4=o4, pA=pA: A.copy(out=xr_sb[:, o4, 0:T], in_=pA[:, 0:T]), [pA_b], [xr_b])
                    K.op("act", lambda o4=o4: A.copy(out=xr_sb[:, o4, T:T + 4], in_=psB[:, o4 * 4:o4 * 4 + 4]), [psB_b], [xr_b])
                    K.op("dve", lambda o4=o4, c=c: V.tensor_scalar(
                        out=xc[:, o4, :], in0=xr_sb[:, o4, 0:T], scalar1=ctap[:, tapi, c, 0:1], scalar2=cbias[:, c:c + 1],
                        op0=ALU.mult, op1=ALU.add), [xr_b, cb], [xc_b])
                    for j in range(1, 5):
                        K.op("dve", lambda o4=o4, c=c, j=j: V.scalar_tensor_tensor(
                            out=xc[:, o4, :], in0=xr_sb[:, o4, j:j + T], scalar=ctap[:, tapi, c, j:j + 1], in1=xc[:, o4, :],
                            op0=ALU.mult, op1=ALU.add), [xr_b, cb, xc_b], [xc_b])
                    K.op("pool", lambda o4=o4: G.tensor_copy(out=xcb[:, o4, :], in_=xc[:, o4, :]), [xc_b], [xcb_b])
                    for d in range(nd):
                        sl = slots[d]
                        pr, pr_b = next_ps()
                        mm_group(pr[:, 0:T], [(gwt[d][0][:, c, :], xcb[:, o4, :])], [gw_b, xcb_b], [pr_b])
                        acc = rs_t[:, c:c + 1] if d == 0 else None
                        K.op("act", lambda d=d, o4=o4, c=c, pr=pr, sl=sl, acc=acc: A.activation(
                            out=Rt[d][:, o4, :], in_=pr[:, 0:T], func=AF.Tanh, bias=gvh[:, sl, 0, c:c + 1], scale=0.5,
                            accum_out=acc), [pr_b, cb], [R_b[d]] + ([rs_b] if d == 0 else []))
                        pi, pi_b = next_ps()
                        mm_group(pi[:, 0:T], [(gwt[d][1][:, c, :], xcb[:, o4, :])], [gw_b, xcb_b], [pi_b])
                        K.op("act", lambda d=d, o4=o4, c=c, pi=pi, sl=sl: A.activation(
                            out=It[d][:, o4, :], in_=pi[:, 0:T], func=AF.Tanh, bias=gvh[:, sl, 1, c:c + 1], scale=0.5),
                            [pi_b, cb], [I_b[d]])
                for d in range(nd):
                    sl = slots[d]
                    for o4 in range(4):
                        c = og * 4 + o4
                        K.op("act", lambda d=d, o4=o4, c=c, sl=sl: A.activation(
                            out=At[d][:, o4, :], in_=Rt[d][:, o4, :], func=AF.Tanh, scale=clq[:, sl, c:c + 1],
                            bias=clq[:, sl, c:c + 1]), [R_b[d], cb], [A_b[d]])
                for d in range(nd):
                    K.op("act", lambda d=d: A.activation(out=Rt[d][:], in_=At[d][:], func=AF.Sqrt, scale=-1.0),
                         [A_b[d]], [R_b[d]])
                    K.op("dve", lambda d=d: V.tensor_scalar(out=Wt[:], in0=At[d][:], scalar1=-1.0, scalar2=1.0, op0=ALU.mult, op1=ALU.add),
                         [A_b[d]], [W_b])
                    K.op("dve", lambda: V.reciprocal(out=Wt[:], in_=Wt[:]), [W_b], [W_b])
                    K.op("dve", lambda d=d: V.scalar_tensor_tensor(out=At[d][:], in0=At[d][:], scalar=1.0, in1=Wt[:],
                                                                   op0=ALU.add, op1=ALU.mult), [A_b[d], W_b], [A_b[d]])
                    K.op("dve", lambda d=d: V.scalar_tensor_tensor(out=It[d][:], in0=It[d][:], scalar=1.0, in1=xc[:],
                                                                   op0=ALU.add, op1=ALU.mult), [I_b[d], xc_b], [I_b[d]])
                    K.op("pool", lambda d=d: G.tensor_tensor(out=It[d][:], in0=It[d][:], in1=Rt[d][:], op=ALU.mult),
                         [I_b[d], R_b[d]], [I_b[d]])
                    K.op("pool", lambda d=d: G.tensor_tensor(out=It[d][:], in0=It[d][:], in1=Wt[:], op=ALU.mult),
                         [I_b[d], W_b], [I_b[d]])
                for o4 in range(4):
                    c = og * 4 + o4
                    K.op("dve", lambda o4=o4, c=c: V.tensor_tensor_scan(
                        out=HF[:, o4, :], data0=At[0][:, o4, :], data1=It[0][:, o4, :], initial=st8[:, c:c + 1],
                        op0=ALU.mult, op1=ALU.add), [A_b[0], I_b[0], st8_b], [HF_b])
                    K.op("dve", lambda o4=o4, c=c: V.tensor_copy(out=st8[:, c:c + 1], in_=HF[:, o4, T - 1:T]), [HF_b], [st8_b])
                if spill_j is not None:
                    j = spill_j
                    K.dma("sp", spsem, hf_s[j, :, og * 4:(og + 1) * 4, :], HF[:], reads=[HF_b], writes=[hf_b[j][og]])
                    K.dma("sp", spsem, ab_s[j, :, og * 4:(og + 1) * 4, :], At[1][:], reads=[A_b[1]], writes=[ab_b[j][og]])
                    K.dma("sp", spsem, ub_s[j, :, og * 4:(og + 1) * 4, :], It[1][:], reads=[I_b[1]], writes=[ub_b[j][og]])
            run_slabs([(w_in, 0, g * 512) for g in range(4)], comp)
            K.op("dve", lambda: V.tensor_tensor(out=rsum[:], in0=rsum[:], in1=rs_t[:], op=ALU.add), [rs_b], [rs_b])

        for s in range(3):
            load_gw([s])
            K.op("dve", lambda: V.memset(st8[:], 0.0), [], [st8_b])
            K.op("dve", lambda: V.memset(rsum[:], 0.0), [], [rs_b])
            for j in range(NT):
                lru_tile(xo[s, j * T:j * T + WIN, :], [s], s, None)
            K.op("dve", lambda: V.tensor_scalar(out=rsum[:], in0=rsum[:], scalar1=0.5, scalar2=0.5 * L, op0=ALU.mult, op1=ALU.add),
                 [rs_b], [rs_b])
            K.op("dve", lambda s=s: V.tensor_tensor(out=rsum[:], in0=rsum[:], in1=cl[:, s, :], op=ALU.mult), [rs_b, cb], [rs_b])
            K.op("act", lambda: A.activation(out=aend[:], in_=rsum[:], func=AF.Exp), [rs_b], [rs_b])
            for dirn in range(2):
                K.op("dve", lambda dirn=dirn: V.tensor_tensor(out=rsum[:], in0=aend[:], in1=carry[:, dirn, :], op=ALU.mult),
                     [rs_b, carry_b], [rs_b])
                K.op("dve", lambda: V.tensor_tensor(out=rsum[:], in0=rsum[:], in1=st8[:], op=ALU.add), [rs_b, st8_b], [rs_b])
                K.op("dve", lambda dirn=dirn: V.tensor_tensor(out=rsum[:], in0=rsum[:], in1=carry[:, dirn, :], op=ALU.subtract),
                     [rs_b, carry_b], [rs_b])
                K.op("dve", lambda dirn=dirn, s=s: V.scalar_tensor_tensor(
                    out=carry[:, dirn, :], in0=rsum[:], scalar=sel[:, s, dirn:dirn + 1], in1=carry[:, dirn, :],
                    op0=ALU.mult, op1=ALU.add), [rs_b, carry_b, cb], [carry_b])
        load_gw([3, 4])
        K.op("dve", lambda: V.tensor_copy(out=st8[:], in_=carry[:, 0, :]), [carry_b], [st8_b])
        for j in range(NT):
            lru_tile(xw[j * T:j * T + WIN, :], [3, 4], 3, j)
        K.barrier()

    with ExitStack() as es:
        h = sbt(es, "h", [128, NCH, WIN], BF16)
        h_b = Buf("h")
        regA = sbt(es, "regA", [128, 64, T], BF16)
        rA_b = [Buf("rA%d" % i) for i in range(64)]
        kT = sbt(es, "kT", [128, 4, WIN], BF16)
        kT_b = Buf("kT")
        vtok = sbt(es, "vtok", [128, NBLK, 512], BF16)
        v_b = Buf("vtok")
        qb_t = [sbt(es, "qb%d" % i, [128, 4, T], BF16) for i in range(2)]
        qb_b = [Buf("qb%d" % i) for i in range(2)]
        cosT = sbt(es, "cosT", [128, WIN])
        nsinT = sbt(es, "nsinT", [128, WIN])
        trig_b = Buf("trig")
        posi = sbt(es, "posi", [128, WIN], I32)
        posf = sbt(es, "posf", [128, WIN])
        possem = K.sem("possem")
        rp1 = sbt(es, "rp1", [128, WIN])
        rp2 = sbt(es, "rp2", [128, WIN])
        rp_b = Buf("rp")
        tg1, tg2, tgi = rp1, rp2, posi
        pT = [sbt(es, "pT%d" % i, [128, 512], BF16) for i in range(6)]
        pT_b = [Buf("pT%d" % i) for i in range(6)]
        rden = sbt(es, "rden", [128, 512])
        rden_b = Buf("rden")
        lbuf = [[sbt(es, "lb%d_%d" % (i, k), [128, T]) for k in range(3)] for i in range(2)]
        lbuf_b = [[Buf("lb%d_%d" % (i, k)) for k in range(3)] for i in range(2)]
        lsem = [K.sem("lsem%d" % i) for i in range(2)]
        hb = sbt(es, "hb", [128, T])
        hb_b = Buf("hb")
        gel = sbt(es, "gel", [128, T])
        gel_b = Buf("gel")
        mo_t = [sbt(es, "mo%d" % i, [128, 512]) for i in range(2)]
        mo_tb = [Buf("mot%d" % i) for i in range(2)]
        mosem = [K.sem("mosem%d" % i) for i in range(2)]
        macc = [sbt(es, "macc%d" % i, [128, T]) for i in range(4)]
        macc_b = [Buf("macc%d" % i) for i in range(4)]
        gtm = [sbt(es, "gtm%d" % i, [128, T]) for i in range(2)]
        gtm_b = [Buf("gtm%d" % i) for i in range(2)]
        stb = sbt(es, "stb", [128, NCH])
        stb_b = Buf("stb")
        K.op("dve", lambda: V.tensor_copy(out=stb[:], in_=carry[:, 1, :]), [carry_b], [stb_b])

        def rope(ps, ps_b_, dst_ap, c0, n):
            K.op("dve", lambda: V.tensor_tensor(out=rp1[0:64, 0:n], in0=ps[64:128, 0:n], in1=nsinT[64:128, c0:c0 + n], op=ALU.mult),
                 [ps_b_, trig_b], [rp_b])
            K.op("dve", lambda: V.tensor_tensor(out=rp1[64:128, 0:n], in0=ps[0:64, 0:n], in1=nsinT[0:64, c0:c0 + n], op=ALU.mult),
                 [ps_b_, trig_b], [rp_b])
            K.op("dve", lambda: V.tensor_tensor(out=rp2[:, 0:n], in0=ps[:, 0:n], in1=cosT[:, c0:c0 + n], op=ALU.mult),
                 [ps_b_, trig_b], [rp_b])
            return lambda wr: K.op("pool", lambda: G.tensor_tensor(out=dst_ap, in0=rp1[:, 0:n], in1=rp2[:, 0:n], op=ALU.add),
                                   [rp_b], wr)

        for j in range(NT - 1, -1, -1):
            t0 = j * T
            make_h(xw[t0:t0 + WIN, :], NBLK, h, h_b, nmp)
            K.dma("sp", possem, posi[:], posw[0, t0:t0 + WIN].partition_broadcast(128), writes=[trig_b])
            K.op("dve", lambda: V.tensor_copy(out=posf[:], in_=posi[:]), [trig_b, rp_b], [trig_b, rp_b])
            for (dst, col, shift) in ((nsinT, 1, 0.0), (cosT, 0, 0.25)):
                K.op("dve", lambda col=col, shift=shift: V.tensor_scalar(
                    out=tg1[:], in0=posf[:], scalar1=freq[:, col:col + 1], scalar2=None, op0=ALU.mult), [trig_b, cb, rp_b], [trig_b, rp_b])
                K.op("dve", lambda shift=shift: V.tensor_scalar(
                    out=tg1[:], in0=tg1[:], scalar1=float(1.0 / (2 * np.pi)), scalar2=shift, op0=ALU.mult, op1=ALU.add),
                    [trig_b, rp_b], [trig_b, rp_b])
                K.op("dve", lambda: V.tensor_copy(out=tgi[:], in_=tg1[:]), [trig_b, rp_b], [trig_b, rp_b])
                K.op("dve", lambda: V.tensor_copy(out=tg2[:], in_=tgi[:]), [trig_b, rp_b], [trig_b, rp_b])
                K.op("dve", lambda: V.tensor_tensor(out=tg1[:], in0=tg1[:], in1=tg2[:], op=ALU.subtract), [trig_b, rp_b], [trig_b, rp_b])
                K.op("dve", lambda: V.tensor_single_scalar(out=tg2[:], in_=tg1[:], scalar=0.5, op=ALU.is_gt), [trig_b, rp_b], [trig_b, rp_b])
                K.op("dve", lambda: V.tensor_tensor(out=tg1[:], in0=tg1[:], in1=tg2[:], op=ALU.subtract), [trig_b, rp_b], [trig_b, rp_b])
                K.op("dve", lambda: V.tensor_single_scalar(out=tg2[:], in_=tg1[:], scalar=-0.5, op=ALU.is_lt), [trig_b, rp_b], [trig_b, rp_b])
                K.op("dve", lambda: V.tensor_tensor(out=tg1[:], in0=tg1[:], in1=tg2[:], op=ALU.add), [trig_b, rp_b], [trig_b, rp_b])
                K.op("act", lambda dst=dst: A.activation(out=dst[:], in_=tg1[:], func=AF.Sin, scale=float(2 * np.pi)),
                     [trig_b, rp_b], [trig_b, rp_b])

            def comp_k(idx, st, sbf):
                for g in range(4):
                    for (c0, n) in [(a, min(512, WIN - a)) for a in range(0, WIN, 512)]:
                        pt, pb = next_ps()
                        mm_group(pt[:, 0:n], [(st[:, k, g * 128:(g + 1) * 128], h[:, k, c0:c0 + n]) for k in range(NCH)],
                                 [sbf, h_b], [pb])
                        fin = rope(pt, pb, kT[:, g, c0:c0 + n], c0, n)
                        fin([kT_b])
            run_slabs([(w_in, 0, 6144)], comp_k)

            def comp_v(idx, st, sbf):
                for blk in range(NBLK):
                    pt, pb = next_ps()
                    mm_group(pt[:, :], [(h[:, k, blk * 128:(blk + 1) * 128], st[:, k, :]) for k in range(NCH)], [sbf, h_b], [pb])
                    K.op("act", lambda blk=blk, pt=pt: A.copy(out=vtok[:, blk, :], in_=pt[:, :]), [pb], [v_b])
            run_slabs([(w_in, 0, 6656)], comp_v)

            def comp_q(g, st, sbf):
                qt, qbb = qb_t[g % 2], qb_b[g % 2]
                for hh in range(4):
                    pt, pb = next_ps()
                    mm_group(pt[:, 0:T], [(st[:, k, hh * 128:(hh + 1) * 128], h[:, k, 128:128 + T]) for k in range(NCH)], [sbf, h_b], [pb])
                    fin = rope(pt, pb, qt[:, hh, :], 128, T)
                    fin([qbb])
                for qb in range(TB):
                    gblk = j * TB + qb
                    pts = []
                    for kk in range(3):
                        kb = qb + kk
                        pt, pb = next_ps()
                        mm_group(pt[:, :], [(kT[:, g, kb * 128:(kb + 1) * 128], qt[:, :, qb * 128:(qb + 1) * 128])], [kT_b, qbb], [pb])
                        pi = (qb * 3 + kk) % 6
                        K.op("act", lambda pt=pt, pi=pi: A.activation(out=pT[pi][:], in_=pt[:, :], func=AF.Exp, scale=float(128 ** -0.5)),
                             [pb], [pT_b[pi]])
                        if kk != 1:
                            mi = (0 if kk == 0 else 1)
                            if kk == 0 and gblk == 0:
                                mi = 2
                            if kk == 2 and gblk == NT * TB - 1:
                                mi = 3
                            K.op("dve", lambda pi=pi, mi=mi: V.tensor_tensor(
                                out=pT[pi][:], in0=pT[pi][:], in1=masks_b[:, mi].rearrange("p a b -> p (a b)"), op=ALU.mult),
                                [pT_b[pi], cb], [pT_b[pi]])
                        pts.append(pi)
                    po, po_b = next_ps()
                    mm_group(po[:, :], [(vtok[:, qb + kk, g * 128:(g + 1) * 128], pT[pts[kk]][:]) for kk in range(3)],
                             [v_b] + [pT_b[p] for p in pts], [po_b])
                    pd, pd_b = next_ps()
                    mm_group(pd[:, :], [(ones_b[:], pT[pts[kk]][:]) for kk in range(3)], [cb] + [pT_b[p] for p in pts], [pd_b])
                    for hh in range(4):
                        K.op("dve", lambda hh=hh, pd=pd: V.tensor_scalar(
                            out=rden[:, hh * 128:(hh + 1) * 128], in0=pd[:, hh * 128:(hh + 1) * 128],
                            scalar1=esink[:, g * 4 + hh:g * 4 + hh + 1], scalar2=None, op0=ALU.add), [pd_b, cb], [rden_b])
                    K.op("dve", lambda: V.reciprocal(out=rden[:], in_=rden[:]), [rden_b], [rden_b])
                    K.op("dve", lambda po=po, qb=qb: V.tensor_tensor(
                        out=regA[:, 16 + g * 4:16 + g * 4 + 4, qb * 128:(qb + 1) * 128],
                        in0=po[:].rearrange("p (a b) -> p a b", a=4), in1=rden[:].rearrange("p (a b) -> p a b", a=4), op=ALU.mult),
                        [po_b, rden_b], [rA_b[16 + g * 4 + hh] for hh in range(4)])
            run_slabs([(w_in, 0, 4096 + g * 512) for g in range(4)], comp_q)

            def comp_qm(hx, st, sbf):
                qt, qbb = qb_t[hx % 2], qb_b[hx % 2]
                for dc in range(4):
                    pt, pb = next_ps()
                    mm_group(pt[:, 0:T], [(st[:, k, dc * 128:(dc + 1) * 128], h[:, k, 128:128 + T]) for k in range(NCH)], [sbf, h_b], [pb])
                    K.op("act", lambda pt=pt, dc=dc: A.copy(out=qt[:, dc, :], in_=pt[:, 0:T]), [pb], [qbb])
                pis = []
                for mb in range(2):
                    pt, pb = next_ps()
                    mm_group(pt[:, 0:T], [(mkT[:, hx * 4 + dc, mb * 128:(mb + 1) * 128], qt[:, dc, :]) for dc in range(4)],
                             [mk_b, qbb], [pb])
                    pi = (hx * 2 + mb) % 6
                    K.op("act", lambda pt=pt, pi=pi: A.activation(out=pT[pi][:, 0:T], in_=pt[:, 0:T], func=AF.Exp, scale=float(512 ** -0.5)),
                         [pb], [pT_b[pi]])
                    pis.append(pi)
                pd, pd_b = next_ps()
                mm_group(pd[:, 0:T], [(ones_b[:], pT[p][:, 0:T]) for p in pis], [cb] + [pT_b[p] for p in pis], [pd_b])
                K.op("dve", lambda pd=pd: V.reciprocal(out=rden[:, 0:T], in_=pd[:, 0:T]), [pd_b], [rden_b])
                for oc in range(4):
                    c = hx * 4 + oc
                    po, po_b = next_ps()
                    mm_group(po[:, 0:T], [(mv[:, mb, c * 128:(c + 1) * 128], pT[pis[mb]][:, 0:T]) for mb in range(2)],
                             [mv_b] + [pT_b[p] for p in pis], [po_b])
                    K.op("dve", lambda po=po, c=c: V.tensor_tensor(out=regA[:, 32 + c, :], in0=po[:, 0:T], in1=rden[:, 0:T], op=ALU.mult),
                         [po_b, rden_b], [rA_b[32 + c]])
            run_slabs([(w_in, 0, 7168 + g * 512) for g in range(4)], comp_qm)

            def comp_gr(og, st, sbf):
                for o4 in range(4):
                    c = og * 4 + o4
                    li = c % 2
                    lb, lbb = lbuf[li], lbuf_b[li]
                    K.dma("sp", lsem[li], lb[0][:], hf_s[j, :, c, :], reads=[hf_b[j][og]], writes=[lbb[0]])
                    K.dma("sp", lsem[li], lb[1][:], ab_s[j, :, c, :], reads=[ab_b[j][og]], writes=[lbb[1]])
                    K.dma("sp", lsem[li], lb[2][:], ub_s[j, :, c, :], reads=[ub_b[j][og]], writes=[lbb[2]])
                    pt, pb = next_ps()
                    mm_group(pt[:, 0:T], [(st[:, k, o4 * 128:(o4 + 1) * 128], h[:, k, 128:128 + T]) for k in range(NCH)], [sbf, h_b], [pb])
                    K.op("act", lambda pt=pt: A.activation(out=gel[:], in_=pt[:, 0:T], func=AF.Gelu), [pb], [gel_b])
                    K.op("dve", lambda lb=lb, c=c: V.tensor_tensor_scan(
                        out=hb[:, ::-1], data0=lb[1][:, ::-1], data1=lb[2][:, ::-1], initial=stb[:, c:c + 1],
                        op0=ALU.mult, op1=ALU.add), [lbb[1], lbb[2], stb_b], [hb_b])
                    K.op("dve", lambda c=c: V.tensor_copy(out=stb[:, c:c + 1], in_=hb[:, 0:1]), [hb_b], [stb_b])
                    K.op("pool", lambda lb=lb: G.tensor_tensor(out=hb[:], in0=hb[:], in1=lb[0][:], op=ALU.add), [hb_b, lbb[0]], [hb_b])
                    K.op("dve", lambda c=c: V.tensor_tensor(out=regA[:, c, :], in0=hb[:], in1=gel[:], op=ALU.mult),
                         [hb_b, gel_b], [rA_b[c]])
            run_slabs([(w_in, 0, 2048 + g * 512) for g in range(4)], comp_gr)

            for cbk in range(4):
                for b in range(3):
                    zt_, zb_ = load_slab(w_br[b], 0, cbk * 512)
                    g_t, g_b = load_slab(w_in, 0, 9216 + b * D + cbk * 512)
                    for o4 in range(4):
                        c = cbk * 4 + o4
                        pz, pz_b = next_ps()
                        mm_group(pz[:, 0:T], [(zt_[:, k, o4 * 128:(o4 + 1) * 128], regA[:, b * 16 + k, :]) for k in range(NCH)],
                                 [zb_] + [rA_b[b * 16 + k] for k in range(NCH)], [pz_b])
                        pg, pg_b = next_ps()
                        mm_group(pg[:, 0:T], [(g_t[:, k, o4 * 128:(o4 + 1) * 128], h[:, k, 128:128 + T]) for k in range(NCH)], [g_b, h_b], [pg_b])
                        gi = (o4 + b) % 2
                        K.op("act", lambda pg=pg, b=b, c=c, gi=gi: A.activation(
                            out=gtm[gi][:], in_=pg[:, 0:T], func=AF.Sigmoid, bias=bgate[:, b * 16 + c:b * 16 + c + 1]),
                            [pg_b, cb], [gtm_b[gi]])
                        if b == 0:
                            K.op("dve", lambda pz=pz, gi=gi, o4=o4: V.tensor_tensor(out=macc[o4][:], in0=gtm[gi][:], in1=pz[:, 0:T], op=ALU.mult),
                                 [gtm_b[gi], pz_b], [macc_b[o4]])
                        else:
                            K.op("dve", lambda pz=pz, gi=gi: V.tensor_tensor(out=gtm[gi][:], in0=gtm[gi][:], in1=pz[:, 0:T], op=ALU.mult),
                                 [gtm_b[gi], pz_b], [gtm_b[gi]])
                            if b == 1:
                                K.op("pool", lambda gi=gi, o4=o4: G.tensor_tensor(out=macc[o4][:], in0=macc[o4][:], in1=gtm[gi][:], op=ALU.add),
                                     [macc_b[o4], gtm_b[gi]], [macc_b[o4]])
                            else:
                                K.op("dve", lambda gi=gi, o4=o4, c=c: V.tensor_tensor(out=regA[:, 48 + c, :], in0=macc[o4][:], in1=gtm[gi][:], op=ALU.add),
                                     [macc_b[o4], gtm_b[gi]], [rA_b[48 + c]])

            def comp_out(cbk, st, sbf):
                for tb in range(TB):
                    pt, pb = next_ps()
                    mm_group(pt[:, :], [(regA[:, 48 + k, tb * 128:(tb + 1) * 128], st[:, k, :]) for k in range(NCH)],
                             [sbf] + [rA_b[48 + k] for k in range(NCH)], [pb])
                    mi = (cbk * TB + tb) % 2
                    K.op("act", lambda pt=pt, mi=mi: A.copy(out=mo_t[mi][:], in_=pt[:, :]), [pb], [mo_tb[mi]])
                    K.dma("sp", mosem[mi], mo_s[t0 + tb * 128:t0 + (tb + 1) * 128, cbk * 512:(cbk + 1) * 512], mo_t[mi][:],
                          reads=[mo_tb[mi]], writes=[mo_b[j][tb]])
            run_slabs([(w_out, 0, g * 512) for g in range(4)], comp_out)
            if DBG and j == 0:
                K.barrier()
                K.dma("sp", mosem[0], dbg_y, regA[:], reads=rA_b, writes=[])
                K.dma("sp", mosem[0], dbg_h, h[:], reads=[h_b], writes=[])
                K.barrier()
        K.barrier()

    with ExitStack() as es:
        gam = [sbt(es, "gam%d" % i, [128, D]) for i in range(3)]
        gam_b = Buf("gam")
        gsem = K.sem("gsem")
        for i in range(3):
            K.dma("sp", gsem, gam[i][:], rows_d[i, :].partition_broadcast(128), writes=[gam_b])
        hm = sbt(es, "hm", [128, NCH, T], BF16)
        hm_b = Buf("hm")
        u = sbt(es, "u", [128, 64, T], BF16)
        u_b = [Buf("u%d" % i) for i in range(64)]
        dd = sbt(es, "dd", [128, TB, D])
        dd_b = [Buf("dd%d" % i) for i in range(TB)]
        x1t = sbt(es, "x1t", [128, D])
        x1t_b = Buf("x1t")
        x1sem = K.sem("x1sem")
        rl = [sbt(es, "rl%d" % i, [128, T]) for i in range(2)]
        rl_b = [Buf("rl%d" % i) for i in range(2)]
        osem = K.sem("osem")

        def norm_scale(src_t, src_b, g_t, dst_t, dst_b, junk_t, junk_b):
            rstd_from(src_t, src_b, junk_t, junk_b)
            K.op("dve", lambda: V.scalar_tensor_tensor(out=dst_t[:], in0=src_t[:], scalar=st_t[:, 2:3], in1=g_t[:],
                                                       op0=ALU.mult, op1=ALU.mult), [src_b, st_b, gam_b], [dst_b])

        for j in range(NT):
            t0 = j * T
            for tb in range(TB):
                r0 = t0 + tb * 128
                K.dma("sp", tok_s[0], tok_t[0][:], mo_s[r0:r0 + 128, :], reads=[mo_b[j][tb]], writes=[tok_b[0]])
                K.dma("sp", tok_s[1], tok_t[1][:], xw[128 + r0:128 + r0 + 128, :], writes=[tok_b[1]])
                norm_scale(tok_t[0], tok_b[0], gam[0], tok_t[0], tok_b[0], xs_t[0], xs_b[0])
                K.op("pool", lambda: G.tensor_tensor(out=x1t[:], in0=tok_t[0][:], in1=tok_t[1][:], op=ALU.add),
                     [tok_b[0], tok_b[1]], [x1t_b])
                K.dma("sp", x1sem, x1_s[r0:r0 + 128, :], x1t[:], reads=[x1t_b], writes=[x1_b[j][tb]])
                rstd_from(x1t, x1t_b, xs_t[0], xs_b[0])
                K.op("dve", lambda: V.scalar_tensor_tensor(out=xs_t[1][:], in0=x1t[:], scalar=st_t[:, 2:3], in1=gam[1][:],
                                                           op0=ALU.mult, op1=ALU.mult), [x1t_b, st_b, gam_b], [xs_b[1]])
                to_featmajor(xs_t[1], xs_b[1], hm, hm_b, tb * 128, None)

            def comp_up(idx, st, sbf):
                for o4 in range(4):
                    c = idx * 4 + o4
                    pt, pb = next_ps()
                    mm_group(pt[:, 0:T], [(st[:, k, o4 * 128:(o4 + 1) * 128], hm[:, k, :]) for k in range(NCH)], [sbf, hm_b], [pb])
                    ri = c % 2
                    K.op("act", lambda pt=pt, ri=ri: A.activation(out=rl[ri][:], in_=pt[:, 0:T], func=AF.Relu), [pb], [rl_b[ri]])
                    K.op("pool", lambda ri=ri, c=c: G.tensor_tensor(out=u[:, c, :], in0=rl[ri][:], in1=rl[ri][:], op=ALU.mult),
                         [rl_b[ri]], [u_b[c]])
            run_slabs([(w_up, 0, g * 512) for g in range(16)], comp_up)

            acc_ps = {}

            def comp_down(idx, st, sbf):
                cbk, kq = idx // 4, idx % 4
                for tb in range(TB):
                    if kq == 0:
                        acc_ps[tb] = next_ps()
                    pt, pb = acc_ps[tb]

                    def fn(pt=pt, tb=tb, kq=kq):
                        ins = None
                        for k in range(NCH):
                            ins = nc.tensor.matmul(pt[:, :], lhsT=u[:, kq * 16 + k, tb * 128:(tb + 1) * 128], rhs=st[:, k, :],
                                                   start=(kq == 0 and k == 0), stop=(kq == 3 and k == NCH - 1))
                        return ins
                    K.op("pe", fn, [sbf] + [u_b[kq * 16 + k] for k in range(NCH)], [pb])
                    if kq == 3:
                        K.op("act", lambda pt=pt, tb=tb, cbk=cbk: A.copy(out=dd[:, tb, cbk * 512:(cbk + 1) * 512], in_=pt[:, :]),
                             [pb], [dd_b[tb]])
            run_slabs([(w_down, kq * D, cbk * 512) for cbk in range(4) for kq in range(4)], comp_down)

            for tb in range(TB):
                r0 = t0 + tb * 128
                K.dma("sp", tok_s[1], tok_t[1][:], x1_s[r0:r0 + 128, :], reads=[x1_b[j][tb]], writes=[tok_b[1]])
                rstd_from(dd[:, tb, :], dd_b[tb], xs_t[0], xs_b[0])
                K.op("dve", lambda tb=tb: V.scalar_tensor_tensor(out=tok_t[0][:], in0=dd[:, tb, :], scalar=st_t[:, 2:3], in1=gam[2][:],
                                                                 op0=ALU.mult, op1=ALU.mult), [dd_b[tb], st_b, gam_b], [tok_b[0]])
                K.op("pool", lambda: G.tensor_tensor(out=tok_t[0][:], in0=tok_t[0][:], in1=tok_t[1][:], op=ALU.add),
                     [tok_b[0], tok_b[1]], [tok_b[0]])
                K.dma("sp", osem, out_d[r0:r0 + 128, :], tok_t[0][:], reads=[tok_b[0]], writes=[])
        K.barrier()
        if DBG:
            K.dma("sp", osem, dbg_x1, x1_s, reads=[], writes=[])
            K.dma("sp", osem, dbg_mo, mo_s, reads=[], writes=[])
            K.dma("sp", osem, dbg_hf, hf_s, reads=[], writes=[])
            K.barrier()
    top.close()
    return nc


_NC_CACHE = {}


def _chan(v):
    v = np.asarray(v, np.float32)
    lead = v.shape[:-1]
    r = v.reshape(lead + (16, 128))
    return np.ascontiguousarray(np.moveaxis(r, -1, 0))


def kernel(x, mem, positions, norm_mix_pre, norm_mix_post, norm_mem, w_in, b_gate, conv_w, conv_b,
           wr_f, br_f, wi_f, bi_f, lam_f, wr_b, br_b, wi_b, bi_b, lam_b, attn_sink, w_mem_kv,
           w_br_lru, w_br_attn, w_br_mem, w_out, norm_mlp_pre, norm_mlp_post, w_up, w_down):
    x = np.asarray(x, np.float32)
    B, S, _ = x.shape
    L = S // 4
    NT = L // T
    LW = L + 256
    if NT not in _NC_CACHE:
        _NC_CACHE[NT] = build(NT)
    nc = _NC_CACHE[NT]
    f = lambda a: np.ascontiguousarray(np.asarray(a, np.float32))
    xpad = np.zeros((B, S + 256, D), np.float32)
    xpad[:, 128:128 + S] = x
    pos = np.asarray(positions, np.int32)
    pospad = np.zeros((B, S + 256), np.int32)
    pospad[:, 128:128 + S] = pos
    cw = f(conv_w)[0]
    z = np.zeros_like(cw[0])
    tap_f = np.stack([z, cw[0], cw[1], cw[2], cw[3]], -1)
    tap_r = np.stack([cw[3], cw[2], cw[1], cw[0], z], -1)
    dirp = {
        "f": (f(wr_f)[0], f(wi_f)[0], f(br_f)[0].reshape(-1), f(bi_f)[0].reshape(-1), f(lam_f)[0]),
        "b": (f(wr_b)[0], f(wi_b)[0], f(br_b)[0].reshape(-1), f(bi_b)[0].reshape(-1), f(lam_b)[0]),
    }
    iu = np.arange(128)
    maskP = (iu[:, None] >= iu[None, :]).astype(np.float32)
    maskN = (iu[:, None] <= iu[None, :]).astype(np.float32)
    zero = np.zeros_like(maskP)
    half = 64
    fr = (10000.0 ** (-(np.arange(half, dtype=np.float32)) / half)).astype(np.float32)
    freq = np.stack([np.concatenate([fr, fr]), np.concatenate([fr, -fr])], -1).astype(np.float32)
    common = {
        "w_in": f(w_in)[0], "w_mem_kv": f(w_mem_kv)[0], "w_br_lru": f(w_br_lru)[0], "w_br_attn": f(w_br_attn)[0],
        "w_br_mem": f(w_br_mem)[0], "w_out": f(w_out)[0], "w_up": f(w_up)[0], "w_down": f(w_down)[0],
        "cbias": _chan(f(conv_b)[0]), "nmp": _chan(f(norm_mix_pre)[0]), "nmem": _chan(f(norm_mem)[0]),
        "bgate": np.ascontiguousarray(f(b_gate)[0].reshape(48, 128).T),
        "rows": np.ascontiguousarray(np.stack([f(norm_mix_post)[0], f(norm_mlp_pre)[0], f(norm_mlp_post)[0]], 0)),
        "sinkb": np.ascontiguousarray(np.broadcast_to(f(attn_sink)[0][None, :], (128, 16))),
        "freq": freq, "ident": np.eye(128, dtype=np.float32),
    }
    in_maps = []
    for c in range(8):
        b, q = c // 4, c % 4
        others = [("f", qq) for qq in range(q)] + [("b", qq) for qq in range(3, q, -1)]
        slots = [o[0] for o in others] + ["f", "b"]
        gw = np.stack([np.stack([np.transpose(dirp[s][0], (1, 0, 2)), np.transpose(dirp[s][1], (1, 0, 2))], 0) for s in slots], 0)
        gv = np.stack([np.stack([_chan(dirp[s][2]), _chan(dirp[s][3]), _chan(dirp[s][4])], 1) for s in slots], 1)
        taps = [tap_f if o[0] == "f" else tap_r for o in others] + [tap_f]
        ctap = np.stack([np.moveaxis(t_.reshape(16, 128, 5), 1, 0) for t_ in taps], 1)
        sel = np.zeros((128, 3, 2), np.float32)
        xo = np.zeros((3, LW, D), np.float32)
        for si, (dr, qq) in enumerate(others):
            win = xpad[b, qq * L:qq * L + LW]
            xo[si] = win if dr == "f" else win[::-1]
            sel[:, si, 0 if dr == "f" else 1] = 1.0
        first = (q == 0)
        last = (q == 3)
        masks = np.stack([maskP, maskN, zero if first else maskP, zero if last else maskN], 0)
        masks = np.ascontiguousarray(np.broadcast_to(masks[:, :, None, :], (4, 128, 4, 128)).transpose(1, 0, 2, 3))
        m = dict(common)
        m.update({
            "xw": np.ascontiguousarray(xpad[b, q * L:q * L + LW]), "xo": xo,
            "posw": np.ascontiguousarray(pospad[b:b + 1, q * L:q * L + LW]),
            "memb": f(mem)[b], "gw": np.ascontiguousarray(gw.astype(np.float32)), "gv": np.ascontiguousarray(gv),
            "ctap": np.ascontiguousarray(ctap), "sel": sel, "masks": masks,
        })
        in_maps.append(m)
    res = run_bass_kernel_spmd(nc, in_maps, core_ids=list(range(8)))
    global LAST_RES
    LAST_RES = res
    out = np.zeros((B, S, D), np.float32)
    for c in range(8):
        b, q = c // 4, c % 4
        out[b, q * L:(q + 1) * L] = res.results[c]["out"]
    return out
```

```python
import os
import numpy as np
from contextlib import ExitStack
import concourse.bass as bass
import concourse.mybir as mybir
from concourse.bass_utils import run_bass_kernel_spmd

F32 = mybir.dt.float32
BF16 = mybir.dt.bfloat16
I32 = mybir.dt.int32
ALU = mybir.AluOpType
AF = mybir.ActivationFunctionType

D = 2048
NCH = 16
T = 256
TB = T // 128
WIN = T + 256
NBLK = WIN // 128
DFF = 8192
NIN = 15360
EPS = 1e-6
NSLAB = 2


class Sem:
    def __init__(self, nc, name, reg):
        self.h = nc.alloc_semaphore(name)
        self.v = 0
        reg.append(self)


class Buf:
    __slots__ = ("name", "w", "r")

    def __init__(self, name):
        self.name = name
        self.w = None
        self.r = {}


class Ctx:
    def __init__(self, nc):
        self.nc = nc
        self.sems = []
        self.eng = {"pe": nc.tensor, "act": nc.scalar, "dve": nc.vector, "pool": nc.gpsimd, "sp": nc.sync}
        self.esem = {k: Sem(nc, "s_" + k, self.sems) for k in ("pe", "act", "dve", "pool")}
        self.seen = {k: {} for k in self.eng}

    def sem(self, name):
        return Sem(self.nc, name, self.sems)

    def _deps(self, reads, writes):
        d = {}

        def add(s, v):
            if d.get(s, 0) < v:
                d[s] = v
        for b in reads:
            if b.w is not None:
                add(*b.w)
        for b in writes:
            if b.w is not None:
                add(*b.w)
            for s, v in b.r.items():
                add(s, v)
        return d

    def _wait(self, e, d):
        seen = self.seen[e]
        for s, v in d.items():
            if seen.get(s, 0) < v:
                self.eng[e].wait_ge(s.h, v)
                seen[s] = v

    def _commit(self, ev, reads, writes):
        s, v = ev
        for b in writes:
            b.w = ev
            b.r = {}
        for b in reads:
            if b.r.get(s, 0) < v:
                b.r[s] = v

    def op(self, e, fn, reads=(), writes=()):
        self._wait(e, self._deps(reads, writes))
        ins = fn()
        s = self.esem[e]
        s.v += 1
        ins.then_inc(s.h, 1)
        self._commit((s, s.v), reads, writes)

    def dma(self, e, sem, out, in_, reads=(), writes=()):
        self._wait(e, self._deps(reads, writes))
        ins = self.eng[e].dma_start(out=out, in_=in_)
        sem.v += 16
        ins.then_inc(sem.h, 16)
        self._commit((sem, sem.v), reads, writes)

    def barrier(self):
        for e in self.eng:
            d = {s: s.v for s in self.sems if s.v > 0}
            self._wait(e, d)


def build(NT):
    L = NT * T
    LW = L + 256
    nc = bass.Bass("TRN2", target_bir_lowering=False)
    K = Ctx(nc)

    def din(name, shape, dt=F32):
        return nc.dram_tensor(name, shape, dt, kind="ExternalInput").ap()

    def dint(name, shape, dt=F32):
        return nc.dram_tensor(name, shape, dt, kind="Internal").ap()

    xw = din("xw", [LW, D])
    xo = din("xo", [3, LW, D])
    posw = din("posw", [1, LW], I32)
    memb = din("memb", [256, D])
    w_in = din("w_in", [D, NIN])
    w_mem_kv = din("w_mem_kv", [D, 2 * D])
    w_br = [din("w_br_lru", [D, D]), din("w_br_attn", [D, D]), din("w_br_mem", [D, D])]
    w_out = din("w_out", [D, D])
    w_up = din("w_up", [D, DFF])
    w_down = din("w_down", [DFF, D])
    gw_d = din("gw", [5, 2, 128, NCH, 128])
    gv_d = din("gv", [128, 5, 3, NCH])
    ctap_d = din("ctap", [128, 4, NCH, 5])
    cbias_d = din("cbias", [128, NCH])
    sel_d = din("sel", [128, 3, 2])
    nmp_d = din("nmp", [128, NCH])
    nmem_d = din("nmem", [128, NCH])
    bgate_d = din("bgate", [128, 48])
    rows_d = din("rows", [3, D])
    sink_d = din("sinkb", [128, NCH])
    freq_d = din("freq", [128, 2])
    masks_d = din("masks", [128, 4, 4, 128])
    ident_d = din("ident", [128, 128])
    out_d = nc.dram_tensor("out", [L, D], F32, kind="ExternalOutput").ap()

    DBG = bool(int(os.environ.get("KDBG", "0")))
    if DBG:
        dbg_x1 = nc.dram_tensor("dbg_x1", [L, D], F32, kind="ExternalOutput").ap()
        dbg_mo = nc.dram_tensor("dbg_mo", [L, D], F32, kind="ExternalOutput").ap()
        dbg_hf = nc.dram_tensor("dbg_hf", [NT, 128, NCH, T], F32, kind="ExternalOutput").ap()
        dbg_y = nc.dram_tensor("dbg_y", [128, 64, T], BF16, kind="ExternalOutput").ap()
        dbg_h = nc.dram_tensor("dbg_h", [128, NCH, WIN], BF16, kind="ExternalOutput").ap()
    hf_s = dint("hf_s", [NT, 128, NCH, T])
    ab_s = dint("ab_s", [NT, 128, NCH, T])
    ub_s = dint("ub_s", [NT, 128, NCH, T])
    mo_s = dint("mo_s", [L, D])
    x1_s = dint("x1_s", [L, D])
    hf_b = [[Buf("hf%d_%d" % (j, g)) for g in range(4)] for j in range(NT)]
    ab_b = [[Buf("ab%d_%d" % (j, g)) for g in range(4)] for j in range(NT)]
    ub_b = [[Buf("ub%d_%d" % (j, g)) for g in range(4)] for j in range(NT)]
    mo_b = [[Buf("mo%d_%d" % (j, t)) for t in range(TB)] for j in range(NT)]
    x1_b = [[Buf("x1%d_%d" % (j, t)) for t in range(TB)] for j in range(NT)]

    top = ExitStack()

    def sbt(es, name, shape, dt=F32):
        return es.enter_context(nc.sbuf_tensor("sb_" + name, shape, dt))

    NPS = 5
    ps_t = [top.enter_context(nc.psum_tensor("ps%d" % i, [128, 512], F32)) for i in range(NPS)]
    ps_b = [Buf("ps%d" % i) for i in range(NPS)]
    pst_t = [top.enter_context(nc.psum_tensor("pst%d" % i, [128, 1024], BF16)) for i in range(2)]
    pst_b = [Buf("pst%d" % i) for i in range(2)]
    psh_t = top.enter_context(nc.psum_tensor("psh", [128, 512], F32))
    psh_b = Buf("psh")
    ps_rr = [0]

    def next_ps():
        i = ps_rr[0] % NPS
        ps_rr[0] += 1
        return ps_t[i], ps_b[i]

    def mm_group(out_ap, pairs, reads, writes):
        def fn():
            n = len(pairs)
            ins = None
            for i, (l, r) in enumerate(pairs):
                ins = nc.tensor.matmul(out_ap, lhsT=l, rhs=r, start=(i == 0), stop=(i == n - 1))
            return ins
        K.op("pe", fn, reads, writes)

    csem = K.sem("csem")
    cb = Buf("consts")
    ident_b = sbt(top, "ident_b", [128, 128], BF16)
    ones_b = sbt(top, "ones_b", [128, 128], BF16)
    masks_b = sbt(top, "masks_b", [128, 4, 4, 128], BF16)
    gv = sbt(top, "gv", [128, 5, 3, NCH])
    gvh = sbt(top, "gvh", [128, 5, 3, NCH])
    cl = sbt(top, "cl", [128, 5, NCH])
    clh = sbt(top, "clh", [128, 5, NCH])
    clq = sbt(top, "clq", [128, 5, NCH])
    ctmp = sbt(top, "ctmp", [128, 5, NCH])
    ctmp2 = sbt(top, "ctmp2", [128, 5, NCH])
    ctap = sbt(top, "ctap", [128, 4, NCH, 5])
    cbias = sbt(top, "cbias", [128, NCH])
    sel = sbt(top, "sel", [128, 3, 2])
    nmp = sbt(top, "nmp", [128, NCH])
    nmem = sbt(top, "nmem", [128, NCH])
    bgate = sbt(top, "bgate", [128, 48])
    esink = sbt(top, "esink", [128, NCH])
    freq = sbt(top, "freq", [128, 2])
    neghalf = sbt(top, "neghalf", [128, 8])
    carry = sbt(top, "carry", [128, 2, NCH])
    mk_b = Buf("mkT")
    mv_b = Buf("mv")
    carry_b = Buf("carry")

    for dst, src in ((gv, gv_d), (ctap, ctap_d), (cbias, cbias_d),
                     (sel, sel_d), (nmp, nmp_d), (nmem, nmem_d), (bgate, bgate_d), (esink, sink_d), (freq, freq_d)):
        K.dma("sp", csem, dst[:], src, writes=[cb])
    V = nc.vector
    A = nc.scalar
    G = nc.gpsimd
    K.op("dve", lambda: V.memset(ones_b[:], 1.0), [], [cb])
    K.op("dve", lambda: V.memset(neghalf[:], -0.5), [], [cb])
    K.op("dve", lambda: V.memset(carry[:], 0.0), [], [carry_b])
    K.op("dve", lambda: V.tensor_scalar(out=gvh[:], in0=gv[:], scalar1=0.5, scalar2=None, op0=ALU.mult), [cb], [cb])
    lam = gv[:, :, 2, :]
    K.op("act", lambda: A.activation(out=ctmp[:], in_=lam, func=AF.Abs), [cb], [cb])
    K.op("act", lambda: A.activation(out=ctmp[:], in_=ctmp[:], func=AF.Exp, scale=-1.0), [cb], [cb])
    K.op("act", lambda: A.activation(out=ctmp[:], in_=ctmp[:], func=AF.Ln, bias=1.0), [cb], [cb])
    K.op("dve", lambda: V.tensor_scalar(out=ctmp2[:], in0=lam, scalar1=-1.0, scalar2=0.0, op0=ALU.mult, op1=ALU.max), [cb], [cb])
    K.op("dve", lambda: V.tensor_tensor(out=ctmp[:], in0=ctmp[:], in1=ctmp2[:], op=ALU.add), [cb], [cb])
    K.op("dve", lambda: V.tensor_scalar(out=cl[:], in0=ctmp[:], scalar1=-8.0, scalar2=None, op0=ALU.mult), [cb], [cb])
    K.op("dve", lambda: V.tensor_scalar(out=clh[:], in0=ctmp[:], scalar1=-4.0, scalar2=None, op0=ALU.mult), [cb], [cb])
    K.op("dve", lambda: V.tensor_scalar(out=clq[:], in0=ctmp[:], scalar1=-2.0, scalar2=None, op0=ALU.mult), [cb], [cb])
    K.op("act", lambda: A.activation(out=esink[:], in_=esink[:], func=AF.Exp), [cb], [cb])

    slab_t = [sbt(top, "slab%d" % i, [128, NCH, 512], BF16) for i in range(NSLAB)]
    slab_b = [Buf("slab%d" % i) for i in range(NSLAB)]
    slab_s = [K.sem("slabsem%d" % i) for i in range(NSLAB)]
    slab_rr = [0]

    pc_specs = [(w_mem_kv, 0, g * 512) for g in range(8)] + [(w_in, 0, g * 512) for g in range(4)]
    pc_specs += [(w_in, 0, 6144), (w_in, 0, 6656)] + [(w_in, 0, 4096 + g * 512) for g in range(4)]
    pc_specs += [(w_in, 0, 7168 + g * 512) for g in range(4)] + [(w_in, 0, 2048 + g * 512) for g in range(4)]
    for cbk_ in range(4):
        for b_ in range(3):
            pc_specs += [(w_br[b_], 0, cbk_ * 512), (w_in, 0, 9216 + b_ * D + cbk_ * 512)]
    pc_specs += [(w_out, 0, g * 512) for g in range(4)] + [(w_up, 0, g * 512) for g in range(16)]
    pc_specs += [(w_down, kq * D, cbk_ * 512) for cbk_ in range(4) for kq in range(4)]
    NSL = len(pc_specs)
    wsc = dint("wsc", [NSL, 128, NCH, 512], BF16)
    wsc_b = [Buf("wsc%d" % i) for i in range(NSL)]
    pcsem = K.sem("pcsem")
    pc_id = {}
    for i_, (Wap_, r0_, c0_) in enumerate(pc_specs):
        pc_id[(id(Wap_), r0_, c0_)] = i_
        K.dma("pool", pcsem, wsc[i_], Wap_[r0_:r0_ + D, c0_:c0_ + 512].rearrange("(c p) j -> p c j", p=128),
              writes=[wsc_b[i_]])

    def load_slab_into(t_, b_, s_, Wap, r0, c0):
        i_ = pc_id[(id(Wap), r0, c0)]
        K.dma("sp", s_, t_[:], wsc[i_], reads=[wsc_b[i_]], writes=[b_])

    def load_slab(Wap, r0, c0, ncols=512):
        i = slab_rr[0] % NSLAB
        slab_rr[0] += 1
        load_slab_into(slab_t[i], slab_b[i], slab_s[i], Wap, r0, c0)
        return slab_t[i], slab_b[i]

    def run_slabs(specs, compute, pre=NSLAB - 1):
        pend = []
        for idx, sp in enumerate(specs):
            pend.append((idx, sp, load_slab(*sp)))
            if len(pend) > pre:
                i0, s0, (t0, b0) = pend.pop(0)
                compute(i0, t0, b0)
        for i0, s0, (t0, b0) in pend:
            compute(i0, t0, b0)

    tok_t = [sbt(top, "tok%d" % i, [128, D]) for i in range(2)]
    tok_b = [Buf("tok%d" % i) for i in range(2)]
    tok_s = [K.sem("toksem%d" % i) for i in range(2)]
    xs_t = [sbt(top, "xs%d" % i, [128, D], BF16) for i in range(2)]
    xs_b = [Buf("xs%d" % i) for i in range(2)]
    st_t = sbt(top, "stats", [128, 8])
    st_b = Buf("stats")
    tok_rr = [0]

    def rstd_from(src_t, src_b, junk_t, junk_b):
        K.op("act", lambda: A.activation(out=junk_t[:], in_=src_t[:], func=AF.Square, accum_out=st_t[:, 0:1]),
             [src_b], [junk_b, st_b])
        K.op("dve", lambda: V.tensor_scalar(out=st_t[:, 1:2], in0=st_t[:, 0:1], scalar1=1.0 / D, scalar2=EPS,
                                            op0=ALU.mult, op1=ALU.add), [st_b], [st_b])
        K.op("act", lambda: A.activation(out=st_t[:, 3:4], in_=st_t[:, 1:2], func=AF.Sqrt), [st_b], [st_b])
        K.op("dve", lambda: V.reciprocal(out=st_t[:, 2:3], in_=st_t[:, 3:4]), [st_b], [st_b])

    def to_featmajor(xsrc_t, xsrc_b, dst_t, dst_b, col0, gam, nrows=128):
        for half in range(2):
            pt, pb = pst_t[half], pst_b[half]

            def fn(half=half, pt=pt):
                ins = None
                for c8 in range(8):
                    c = half * 8 + c8
                    ins = nc.tensor.transpose(out=pt[:, c8 * 128:(c8 + 1) * 128], in_=xsrc_t[:, c * 128:(c + 1) * 128],
                                              identity=ident_b[:])
                return ins
            K.op("pe", fn, [xsrc_b, cb], [pb])
            if gam is None:
                K.op("act", lambda half=half, pt=pt: A.copy(
                    out=dst_t[:, half * 8:(half + 1) * 8, col0:col0 + 128],
                    in_=pt[:].rearrange("p (c t) -> p c t", c=8)), [pb], [dst_b])
            else:
                for c8 in range(8):
                    c = half * 8 + c8
                    e = "dve"
                    if e == "act":
                        K.op("act", lambda c=c, c8=c8, pt=pt: A.activation(
                            out=dst_t[:, c, col0:col0 + 128], in_=pt[:, c8 * 128:(c8 + 1) * 128],
                            func=AF.Copy, scale=gam[:, c:c + 1]), [pb, cb], [dst_b])
                    else:
                        K.op("dve", lambda c=c, c8=c8, pt=pt: V.tensor_scalar(
                            out=dst_t[:, c, col0:col0 + 128], in0=pt[:, c8 * 128:(c8 + 1) * 128],
                            scalar1=gam[:, c:c + 1], scalar2=None, op0=ALU.mult), [pb, cb], [dst_b])

    def make_h(src_rows, nblk, dst_t, dst_b, gam):
        for blk in range(nblk):
            i = tok_rr[0] % 2
            tok_rr[0] += 1
            K.dma("sp", tok_s[i], tok_t[i][:], src_rows[blk * 128:(blk + 1) * 128, :], writes=[tok_b[i]])
            rstd_from(tok_t[i], tok_b[i], xs_t[i], xs_b[i])
            K.op("dve", lambda i=i: V.tensor_scalar(out=xs_t[i][:], in0=tok_t[i][:], scalar1=st_t[:, 2:3], scalar2=None,
                                                    op0=ALU.mult), [tok_b[i], st_b], [xs_b[i]])
            to_featmajor(xs_t[i], xs_b[i], dst_t, dst_b, blk * 128, gam)

    with ExitStack() as ctemp:
        ident_f = sbt(ctemp, "ident_f", [128, 128])
        masks_f = sbt(ctemp, "masks_f", [128, 4, 4, 128])
        K.dma("sp", csem, ident_f[:], ident_d, writes=[cb])
        K.dma("sp", csem, masks_f[:], masks_d, writes=[cb])
        K.op("dve", lambda: V.tensor_copy(out=ident_b[:], in_=ident_f[:]), [cb], [cb])
        K.op("dve", lambda: V.tensor_copy(out=masks_b[:], in_=masks_f[:]), [cb], [cb])
        K.barrier()
    mid = ExitStack()
    mkT = sbt(mid, "mkT", [128, NCH, 256], BF16)
    mv = sbt(mid, "mv", [128, 2, D], BF16)

    with ExitStack() as es:
        mnT = sbt(es, "mnT", [128, NCH, 256], BF16)
        mn_b = Buf("mnT")
        make_h(memb, 2, mnT, mn_b, nmem)

        def comp_mk(idx, st, sbf):
            for o4 in range(4):
                c = idx * 4 + o4
                pt, pb = next_ps()
                mm_group(pt[:, 0:256], [(st[:, k, o4 * 128:(o4 + 1) * 128], mnT[:, k, :]) for k in range(NCH)],
                         [sbf, mn_b], [pb])
                K.op("act", lambda c=c, pt=pt: A.copy(out=mkT[:, c, :], in_=pt[:, 0:256]), [pb], [mk_b])
        run_slabs([(w_mem_kv, 0, g * 512) for g in range(4)], comp_mk)

        def comp_mv(idx, st, sbf):
            for blk in range(2):
                pt, pb = next_ps()
                mm_group(pt[:, :], [(mnT[:, k, blk * 128:(blk + 1) * 128], st[:, k, :]) for k in range(NCH)],
                         [sbf, mn_b], [pb])
                K.op("act", lambda blk=blk, pt=pt, idx=idx: A.copy(out=mv[:, blk, idx * 512:(idx + 1) * 512], in_=pt[:, :]),
                     [pb], [mv_b])
        run_slabs([(w_mem_kv, 0, D + g * 512) for g in range(4)], comp_mv)
        K.barrier()

    with ExitStack() as es:
        hL = sbt(es, "hL", [128, NCH, WIN], BF16)
        hL_b = Buf("hL")
        gwt = [[sbt(es, "gw%d_%d" % (d, ri), [128, NCH, 128], BF16) for ri in range(2)] for d in range(2)]
        gw_b = Buf("gw")
        gwsem = K.sem("gwsem")
        xr_sb = sbt(es, "xr_sb", [128, 4, T + 4])
        xr_b = Buf("xr_sb")
        xc = sbt(es, "xc", [128, 4, T])
        xc_b = Buf("xc")
        xcb = sbt(es, "xcb", [128, 4, T], BF16)
        xcb_b = Buf("xcb")
        Rt = [sbt(es, "R%d" % d, [128, 4, T]) for d in range(2)]
        It = [sbt(es, "I%d" % d, [128, 4, T]) for d in range(2)]
        At = [sbt(es, "A%d" % d, [128, 4, T]) for d in range(2)]
        R_b = [Buf("R%d" % d) for d in range(2)]
        I_b = [Buf("I%d" % d) for d in range(2)]
        A_b = [Buf("A%d" % d) for d in range(2)]
        Wt = sbt(es, "Wt", [128, 4, T])
        W_b = Buf("Wt")
        HF = sbt(es, "HF", [128, 4, T])
        HF_b = Buf("HF")
        spsem = K.sem("spsem")
        st8 = sbt(es, "st8", [128, NCH])
        st8_b = Buf("st8")
        rsum = sbt(es, "rsum", [128, NCH])
        rs_t = sbt(es, "rs_t", [128, NCH])
        rs_b = Buf("rsum")
        aend = sbt(es, "aend", [128, NCH])

        xsl_t = [slab_t[0], slab_t[1], sbt(es, "xsl2", [128, NCH, 512], BF16), sbt(es, "xsl3", [128, NCH, 512], BF16)]
        xsl_b = [slab_b[0], slab_b[1], Buf("xsl2"), Buf("xsl3")]
        xsl_s = [slab_s[0], slab_s[1], K.sem("xslsem2"), K.sem("xslsem3")]
        for g_ in range(4):
            load_slab_into(xsl_t[g_], xsl_b[g_], xsl_s[g_], w_in, 0, g_ * 512)

        def load_gw(slots):
            for d, sl in enumerate(slots):
                for ri in range(2):
                    K.dma("pool", gwsem, gwt[d][ri][:], gw_d[sl, ri], writes=[gw_b])

        def lru_tile(src_rows, slots, tapi, spill_j):
            nd = len(slots)
            make_h(src_rows, NBLK, hL, hL_b, nmp)

            def comp(og, st, sbf):
                psB, psB_b = psh_t, psh_b
                for o4 in range(4):
                    c = og * 4 + o4
                    pA, pA_b = next_ps()
                    lw = [st[:, k, o4 * 128:(o4 + 1) * 128] for k in range(NCH)]
                    mm_group(pA[:, 0:T], [(lw[k], hL[:, k, 126:126 + T]) for k in range(NCH)], [sbf, hL_b], [pA_b])
                    mm_group(psB[:, o4 * 4:o4 * 4 + 4], [(lw[k], hL[:, k, 126 + T:130 + T]) for k in range(NCH)], [sbf, hL_b], [psB_b])
                    K.op("act", lambda o4=o4, pA=pA: A.copy(out=xr_sb[:, o4, 0:T], in_=pA[:, 0:T]), [pA_b], [xr_b])
                    K.op("act", lambda o4=o4: A.copy(out=xr_sb[:, o4, T:T + 4], in_=psB[:, o4 * 4:o4 * 4 + 4]), [psB_b], [xr_b])
                    K.op("dve", lambda o4=o4, c=c: V.tensor_scalar(
                        out=xc[:, o4, :], in0=xr_sb[:, o4, 0:T], scalar1=ctap[:, tapi, c, 0:1], scalar2=cbias[:, c:c + 1],
                        op0=ALU.mult, op1=ALU.add), [xr_b, cb], [xc_b])
                    for j in range(1, 5):
                        K.op("dve", lambda o4=o4, c=c, j=j: V.scalar_tensor_tensor(
                            out=xc[:, o4, :], in0=xr_sb[:, o4, j:j + T], scalar=ctap[:, tapi, c, j:j + 1], in1=xc[:, o4, :],
                            op0=ALU.mult, op1=ALU.add), [xr_b, cb, xc_b], [xc_b])
                    K.op("pool", lambda o4=o4: G.tensor_copy(out=xcb[:, o4, :], in_=xc[:, o4, :]), [xc_b], [xcb_b])
                    for d in range(nd):
                        sl = slots[d]
                        pr, pr_b = next_ps()
                        mm_group(pr[:, 0:T], [(gwt[d][0][:, c, :], xcb[:, o4, :])], [gw_b, xcb_b], [pr_b])
                        acc = rs_t[:, c:c + 1] if d == 0 else None
                        K.op("act", lambda d=d, o4=o4, c=c, pr=pr, sl=sl, acc=acc: A.activation(
                            out=Rt[d][:, o4, :], in_=pr[:, 0:T], func=AF.Tanh, bias=gvh[:, sl, 0, c:c + 1], scale=0.5,
                            accum_out=acc), [pr_b, cb], [R_b[d]] + ([rs_b] if d == 0 else []))
                        pi, pi_b = next_ps()
                        mm_group(pi[:, 0:T], [(gwt[d][1][:, c, :], xcb[:, o4, :])], [gw_b, xcb_b], [pi_b])
                        K.op("act", lambda d=d, o4=o4, c=c, pi=pi, sl=sl: A.activation(
                            out=It[d][:, o4, :], in_=pi[:, 0:T], func=AF.Tanh, bias=gvh[:, sl, 1, c:c + 1], scale=0.5),
                            [pi_b, cb], [I_b[d]])
                for d in range(nd):
                    sl = slots[d]
                    for o4 in range(4):
                        c = og * 4 + o4
                        K.op("act", lambda d=d, o4=o4, c=c, sl=sl: A.activation(
                            out=At[d][:, o4, :], in_=Rt[d][:, o4, :], func=AF.Tanh, scale=clq[:, sl, c:c + 1],
                            bias=clq[:, sl, c:c + 1]), [R_b[d], cb], [A_b[d]])
                for d in range(nd):
                    K.op("act", lambda d=d: A.activation(out=Rt[d][:], in_=At[d][:], func=AF.Sqrt, scale=-1.0),
                         [A_b[d]], [R_b[d]])
                    K.op("dve", lambda d=d: V.tensor_scalar(out=Wt[:], in0=At[d][:], scalar1=-1.0, scalar2=1.0, op0=ALU.mult, op1=ALU.add),
                         [A_b[d]], [W_b])
                    K.op("dve", lambda: V.reciprocal(out=Wt[:], in_=Wt[:]), [W_b], [W_b])
                    K.op("dve", lambda d=d: V.scalar_tensor_tensor(out=At[d][:], in0=At[d][:], scalar=1.0, in1=Wt[:],
                                                                   op0=ALU.add, op1=ALU.mult), [A_b[d], W_b], [A_b[d]])
                    K.op("dve", lambda d=d: V.scalar_tensor_tensor(out=It[d][:], in0=It[d][:], scalar=1.0, in1=xc[:],
                                                                   op0=ALU.add, op1=ALU.mult), [I_b[d], xc_b], [I_b[d]])
                    K.op("pool", lambda d=d: G.tensor_tensor(out=It[d][:], in0=It[d][:], in1=Rt[d][:], op=ALU.mult),
                         [I_b[d], R_b[d]], [I_b[d]])
                    K.op("pool", lambda d=d: G.tensor_tensor(out=It[d][:], in0=It[d][:], in1=Wt[:], op=ALU.mult),
                         [I_b[d], W_b], [I_b[d]])
                for o4 in range(4):
                    c = og * 4 + o4
                    K.op("dve", lambda o4=o4, c=c: V.tensor_tensor_scan(
                        out=HF[:, o4, :], data0=At[0][:, o4, :], data1=It[0][:, o4, :], initial=st8[:, c:c + 1],
                        op0=ALU.mult, op1=ALU.add), [A_b[0], I_b[0], st8_b], [HF_b])
                    K.op("dve", lambda o4=o4, c=c: V.tensor_copy(out=st8[:, c:c + 1], in_=HF[:, o4, T - 1:T]), [HF_b], [st8_b])
                if spill_j is not None:
                    j = spill_j
                    K.dma("sp", spsem, hf_s[j, :, og * 4:(og + 1) * 4, :], HF[:], reads=[HF_b], writes=[hf_b[j][og]])
                    K.dma("sp", spsem, ab_s[j, :, og * 4:(og + 1) * 4, :], At[1][:], reads=[A_b[1]], writes=[ab_b[j][og]])
                    K.dma("sp", spsem, ub_s[j, :, og * 4:(og + 1) * 4, :], It[1][:], reads=[I_b[1]], writes=[ub_b[j][og]])
            for g_ in range(4):
                comp(g_, xsl_t[g_], xsl_b[g_])
            K.op("dve", lambda: V.tensor_tensor(out=rsum[:], in0=rsum[:], in1=rs_t[:], op=ALU.add), [rs_b], [rs_b])

        for s in range(3):
            load_gw([s])
            K.op("dve", lambda: V.memset(st8[:], 0.0), [], [st8_b])
            K.op("dve", lambda: V.memset(rsum[:], 0.0), [], [rs_b])
            for j in range(NT):
                lru_tile(xo[s, j * T:j * T + WIN, :], [s], s, None)
            K.op("dve", lambda: V.tensor_scalar(out=rsum[:], in0=rsum[:], scalar1=0.5, scalar2=0.5 * L, op0=ALU.mult, op1=ALU.add),
                 [rs_b], [rs_b])
            K.op("dve", lambda s=s: V.tensor_tensor(out=rsum[:], in0=rsum[:], in1=cl[:, s, :], op=ALU.mult), [rs_b, cb], [rs_b])
            K.op("act", lambda: A.activation(out=aend[:], in_=rsum[:], func=AF.Exp), [rs_b], [rs_b])
            for dirn in range(2):
                K.op("dve", lambda dirn=dirn: V.tensor_tensor(out=rsum[:], in0=aend[:], in1=carry[:, dirn, :], op=ALU.mult),
                     [rs_b, carry_b], [rs_b])
                K.op("dve", lambda: V.tensor_tensor(out=rsum[:], in0=rsum[:], in1=st8[:], op=ALU.add), [rs_b, st8_b], [rs_b])
                K.op("dve", lambda dirn=dirn: V.tensor_tensor(out=rsum[:], in0=rsum[:], in1=carry[:, dirn, :], op=ALU.subtract),
                     [rs_b, carry_b], [rs_b])
                K.op("dve", lambda dirn=dirn, s=s: V.scalar_tensor_tensor(
                    out=carry[:, dirn, :], in0=rsum[:], scalar=sel[:, s, dirn:dirn + 1], in1=carry[:, dirn, :],
                    op0=ALU.mult, op1=ALU.add), [rs_b, carry_b, cb], [carry_b])
        load_gw([3, 4])
        K.op("dve", lambda: V.tensor_copy(out=st8[:], in_=carry[:, 0, :]), [carry_b], [st8_b])
        for j in range(NT):
            lru_tile(xw[j * T:j * T + WIN, :], [3, 4], 3, j)
        K.barrier()

    with ExitStack() as es:
        h = sbt(es, "h", [128, NCH, WIN], BF16)
        h_b = Buf("h")
        regA = sbt(es, "regA", [128, 64, T], BF16)
        rA_b = [Buf("rA%d" % i) for i in range(64)]
        kT = sbt(es, "kT", [128, 4, WIN], BF16)
        kT_b = Buf("kT")
        vtok = sbt(es, "vtok", [128, NBLK, 512], BF16)
        v_b = Buf("vtok")
        qb_t = [sbt(es, "qb%d" % i, [128, 4, T], BF16) for i in range(2)]
        qb_b = [Buf("qb%d" % i) for i in range(2)]
        cosT = sbt(es, "cosT", [128, WIN])
        nsinT = sbt(es, "nsinT", [128, WIN])
        trig_b = Buf("trig")
        posi = sbt(es, "posi", [128, WIN], I32)
        posf = sbt(es, "posf", [128, WIN])
        possem = K.sem("possem")
        rp1 = sbt(es, "rp1", [128, WIN])
        rp2 = sbt(es, "rp2", [128, WIN])
        rp_b = Buf("rp")
        tg1, tg2, tgi = rp1, rp2, posi
        pT = [sbt(es, "pT%d" % i, [128, 512], BF16) for i in range(6)]
        pT_b = [Buf("pT%d" % i) for i in range(6)]
        rden = sbt(es, "rden", [128, 512])
        rden_b = Buf("rden")
        lbuf = [[sbt(es, "lb%d_%d" % (i, k), [128, T]) for k in range(3)] for i in range(2)]
        lbuf_b = [[Buf("lb%d_%d" % (i, k)) for k in range(3)] for i in range(2)]
        lsem = [K.sem("lsem%d" % i) for i in range(2)]
        hb = sbt(es, "hb", [128, T])
        hb_b = Buf("hb")
        gel = sbt(es, "gel", [128, T])
        gel_b = Buf("gel")
        mo_t = [sbt(es, "mo%d" % i, [128, 512]) for i in range(2)]
        mo_tb = [Buf("mot%d" % i) for i in range(2)]
        mosem = [K.sem("mosem%d" % i) for i in range(2)]
        macc = [sbt(es, "macc%d" % i, [128, T]) for i in range(4)]
        macc_b = [Buf("macc%d" % i) for i in range(4)]
        gtm = [sbt(es, "gtm%d" % i, [128, T]) for i in range(2)]
        gtm_b = [Buf("gtm%d" % i) for i in range(2)]
        stb = sbt(es, "stb", [128, NCH])
        stb_b = Buf("stb")
        K.op("dve", lambda: V.tensor_copy(out=stb[:], in_=carry[:, 1, :]), [carry_b], [stb_b])

        def rope(ps, ps_b_, dst_ap, c0, n):
            K.op("dve", lambda: V.tensor_tensor(out=rp1[0:64, 0:n], in0=ps[64:128, 0:n], in1=nsinT[64:128, c0:c0 + n], op=ALU.mult),
                 [ps_b_, trig_b], [rp_b])
            K.op("dve", lambda: V.tensor_tensor(out=rp1[64:128, 0:n], in0=ps[0:64, 0:n], in1=nsinT[0:64, c0:c0 + n], op=ALU.mult),
                 [ps_b_, trig_b], [rp_b])
            K.op("dve", lambda: V.tensor_tensor(out=rp2[:, 0:n], in0=ps[:, 0:n], in1=cosT[:, c0:c0 + n], op=ALU.mult),
                 [ps_b_, trig_b], [rp_b])
            return lambda wr: K.op("pool", lambda: G.tensor_tensor(out=dst_ap, in0=rp1[:, 0:n], in1=rp2[:, 0:n], op=ALU.add),
                                   [rp_b], wr)

        for j in range(NT - 1, -1, -1):
            t0 = j * T
            make_h(xw[t0:t0 + WIN, :], NBLK, h, h_b, nmp)
            K.dma("sp", possem, posi[:], posw[0, t0:t0 + WIN].partition_broadcast(128), writes=[trig_b])
            K.op("dve", lambda: V.tensor_copy(out=posf[:], in_=posi[:]), [trig_b, rp_b], [trig_b, rp_b])
            for (dst, col, shift) in ((nsinT, 1, 0.0), (cosT, 0, 0.25)):
                K.op("dve", lambda col=col, shift=shift: V.tensor_scalar(
                    out=tg1[:], in0=posf[:], scalar1=freq[:, col:col + 1], scalar2=None, op0=ALU.mult), [trig_b, cb, rp_b], [trig_b, rp_b])
                K.op("dve", lambda shift=shift: V.tensor_scalar(
                    out=tg1[:], in0=tg1[:], scalar1=float(1.0 / (2 * np.pi)), scalar2=shift, op0=ALU.mult, op1=ALU.add),
                    [trig_b, rp_b], [trig_b, rp_b])
                K.op("dve", lambda: V.tensor_copy(out=tgi[:], in_=tg1[:]), [trig_b, rp_b], [trig_b, rp_b])
                K.op("dve", lambda: V.tensor_copy(out=tg2[:], in_=tgi[:]), [trig_b, rp_b], [trig_b, rp_b])
                K.op("dve", lambda: V.tensor_tensor(out=tg1[:], in0=tg1[:], in1=tg2[:], op=ALU.subtract), [trig_b, rp_b], [trig_b, rp_b])
                K.op("dve", lambda: V.tensor_single_scalar(out=tg2[:], in_=tg1[:], scalar=0.5, op=ALU.is_gt), [trig_b, rp_b], [trig_b, rp_b])
                K.op("dve", lambda: V.tensor_tensor(out=tg1[:], in0=tg1[:], in1=tg2[:], op=ALU.subtract), [trig_b, rp_b], [trig_b, rp_b])
                K.op("dve", lambda: V.tensor_single_scalar(out=tg2[:], in_=tg1[:], scalar=-0.5, op=ALU.is_lt), [trig_b, rp_b], [trig_b, rp_b])
                K.op("dve", lambda: V.tensor_tensor(out=tg1[:], in0=tg1[:], in1=tg2[:], op=ALU.add), [trig_b, rp_b], [trig_b, rp_b])
                K.op("act", lambda dst=dst: A.activation(out=dst[:], in_=tg1[:], func=AF.Sin, scale=float(2 * np.pi)),
                     [trig_b, rp_b], [trig_b, rp_b])

            def comp_k(idx, st, sbf):
                for g in range(4):
                    for (c0, n) in [(a, min(512, WIN - a)) for a in range(0, WIN, 512)]:
                        pt, pb = next_ps()
                        mm_group(pt[:, 0:n], [(st[:, k, g * 128:(g + 1) * 128], h[:, k, c0:c0 + n]) for k in range(NCH)],
                                 [sbf, h_b], [pb])
                        fin = rope(pt, pb, kT[:, g, c0:c0 + n], c0, n)
                        fin([kT_b])
            run_slabs([(w_in, 0, 6144)], comp_k)

            def comp_v(idx, st, sbf):
                for blk in range(NBLK):
                    pt, pb = next_ps()
                    mm_group(pt[:, :], [(h[:, k, blk * 128:(blk + 1) * 128], st[:, k, :]) for k in range(NCH)], [sbf, h_b], [pb])
                    K.op("act", lambda blk=blk, pt=pt: A.copy(out=vtok[:, blk, :], in_=pt[:, :]), [pb], [v_b])
            run_slabs([(w_in, 0, 6656)], comp_v)

            def comp_q(g, st, sbf):
                qt, qbb = qb_t[g % 2], qb_b[g % 2]
                for hh in range(4):
                    pt, pb = next_ps()
                    mm_group(pt[:, 0:T], [(st[:, k, hh * 128:(hh + 1) * 128], h[:, k, 128:128 + T]) for k in range(NCH)], [sbf, h_b], [pb])
                    fin = rope(pt, pb, qt[:, hh, :], 128, T)
                    fin([qbb])
                for qb in range(TB):
                    gblk = j * TB + qb
                    pts = []
                    for kk in range(3):
                        kb = qb + kk
                        pt, pb = next_ps()
                        mm_group(pt[:, :], [(kT[:, g, kb * 128:(kb + 1) * 128], qt[:, :, qb * 128:(qb + 1) * 128])], [kT_b, qbb], [pb])
                        pi = (qb * 3 + kk) % 6
                        K.op("act", lambda pt=pt, pi=pi: A.activation(out=pT[pi][:], in_=pt[:, :], func=AF.Exp, scale=float(128 ** -0.5)),
                             [pb], [pT_b[pi]])
                        if kk != 1:
                            mi = (0 if kk == 0 else 1)
                            if kk == 0 and gblk == 0:
                                mi = 2
                            if kk == 2 and gblk == NT * TB - 1:
                                mi = 3
                            K.op("dve", lambda pi=pi, mi=mi: V.tensor_tensor(
                                out=pT[pi][:], in0=pT[pi][:], in1=masks_b[:, mi].rearrange("p a b -> p (a b)"), op=ALU.mult),
                                [pT_b[pi], cb], [pT_b[pi]])
                        pts.append(pi)
                    po, po_b = next_ps()
                    mm_group(po[:, :], [(vtok[:, qb + kk, g * 128:(g + 1) * 128], pT[pts[kk]][:]) for kk in range(3)],
                             [v_b] + [pT_b[p] for p in pts], [po_b])
                    pd, pd_b = next_ps()
                    mm_group(pd[:, :], [(ones_b[:], pT[pts[kk]][:]) for kk in range(3)], [cb] + [pT_b[p] for p in pts], [pd_b])
                    for hh in range(4):
                        K.op("dve", lambda hh=hh, pd=pd: V.tensor_scalar(
                            out=rden[:, hh * 128:(hh + 1) * 128], in0=pd[:, hh * 128:(hh + 1) * 128],
                            scalar1=esink[:, g * 4 + hh:g * 4 + hh + 1], scalar2=None, op0=ALU.add), [pd_b, cb], [rden_b])
                    K.op("dve", lambda: V.reciprocal(out=rden[:], in_=rden[:]), [rden_b], [rden_b])
                    K.op("dve", lambda po=po, qb=qb: V.tensor_tensor(
                        out=regA[:, 16 + g * 4:16 + g * 4 + 4, qb * 128:(qb + 1) * 128],
                        in0=po[:].rearrange("p (a b) -> p a b", a=4), in1=rden[:].rearrange("p (a b) -> p a b", a=4), op=ALU.mult),
                        [po_b, rden_b], [rA_b[16 + g * 4 + hh] for hh in range(4)])
            run_slabs([(w_in, 0, 4096 + g * 512) for g in range(4)], comp_q)

            def comp_qm(hx, st, sbf):
                qt, qbb = qb_t[hx % 2], qb_b[hx % 2]
                for dc in range(4):
                    pt, pb = next_ps()
                    mm_group(pt[:, 0:T], [(st[:, k, dc * 128:(dc + 1) * 128], h[:, k, 128:128 + T]) for k in range(NCH)], [sbf, h_b], [pb])
                    K.op("act", lambda pt=pt, dc=dc: A.copy(out=qt[:, dc, :], in_=pt[:, 0:T]), [pb], [qbb])
                pis = []
                for mb in range(2):
                    pt, pb = next_ps()
                    mm_group(pt[:, 0:T], [(mkT[:, hx * 4 + dc, mb * 128:(mb + 1) * 128], qt[:, dc, :]) for dc in range(4)],
                             [mk_b, qbb], [pb])
                    pi = (hx * 2 + mb) % 6
                    K.op("act", lambda pt=pt, pi=pi: A.activation(out=pT[pi][:, 0:T], in_=pt[:, 0:T], func=AF.Exp, scale=float(512 ** -0.5)),
                         [pb], [pT_b[pi]])
                    pis.append(pi)
                pd, pd_b = next_ps()
                mm_group(pd[:, 0:T], [(ones_b[:], pT[p][:, 0:T]) for p in pis], [cb] + [pT_b[p] for p in pis], [pd_b])
                K.op("dve", lambda pd=pd: V.reciprocal(out=rden[:, 0:T], in_=pd[:, 0:T]), [pd_b], [rden_b])
                for oc in range(4):
                    c = hx * 4 + oc
                    po, po_b = next_ps()
                    mm_group(po[:, 0:T], [(mv[:, mb, c * 128:(c + 1) * 128], pT[pis[mb]][:, 0:T]) for mb in range(2)],
                             [mv_b] + [pT_b[p] for p in pis], [po_b])
                    K.op("dve", lambda po=po, c=c: V.tensor_tensor(out=regA[:, 32 + c, :], in0=po[:, 0:T], in1=rden[:, 0:T], op=ALU.mult),
                         [po_b, rden_b], [rA_b[32 + c]])
            run_slabs([(w_in, 0, 7168 + g * 512) for g in range(4)], comp_qm)

            def comp_gr(og, st, sbf):
                for o4 in range(4):
                    c = og * 4 + o4
                    li = c % 2
                    lb, lbb = lbuf[li], lbuf_b[li]
                    K.dma("sp", lsem[li], lb[0][:], hf_s[j, :, c, :], reads=[hf_b[j][og]], writes=[lbb[0]])
                    K.dma("sp", lsem[li], lb[1][:], ab_s[j, :, c, :], reads=[ab_b[j][og]], writes=[lbb[1]])
                    K.dma("sp", lsem[li], lb[2][:], ub_s[j, :, c, :], reads=[ub_b[j][og]], writes=[lbb[2]])
                    pt, pb = next_ps()
                    mm_group(pt[:, 0:T], [(st[:, k, o4 * 128:(o4 + 1) * 128], h[:, k, 128:128 + T]) for k in range(NCH)], [sbf, h_b], [pb])
                    K.op("act", lambda pt=pt: A.activation(out=gel[:], in_=pt[:, 0:T], func=AF.Gelu), [pb], [gel_b])
                    K.op("dve", lambda lb=lb, c=c: V.tensor_tensor_scan(
                        out=hb[:, ::-1], data0=lb[1][:, ::-1], data1=lb[2][:, ::-1], initial=stb[:, c:c + 1],
                        op0=ALU.mult, op1=ALU.add), [lbb[1], lbb[2], stb_b], [hb_b])
                    K.op("dve", lambda c=c: V.tensor_copy(out=stb[:, c:c + 1], in_=hb[:, 0:1]), [hb_b], [stb_b])
                    K.op("pool", lambda lb=lb: G.tensor_tensor(out=hb[:], in0=hb[:], in1=lb[0][:], op=ALU.add), [hb_b, lbb[0]], [hb_b])
                    K.op("dve", lambda c=c: V.tensor_tensor(out=regA[:, c, :], in0=hb[:], in1=gel[:], op=ALU.mult),
                         [hb_b, gel_b], [rA_b[c]])
            run_slabs([(w_in, 0, 2048 + g * 512) for g in range(4)], comp_gr)

            for cbk in range(4):
                for b in range(3):
                    zt_, zb_ = load_slab(w_br[b], 0, cbk * 512)
                    g_t, g_b = load_slab(w_in, 0, 9216 + b * D + cbk * 512)
                    for o4 in range(4):
                        c = cbk * 4 + o4
                        pz, pz_b = next_ps()
                        mm_group(pz[:, 0:T], [(zt_[:, k, o4 * 128:(o4 + 1) * 128], regA[:, b * 16 + k, :]) for k in range(NCH)],
                                 [zb_] + [rA_b[b * 16 + k] for k in range(NCH)], [pz_b])
                        pg, pg_b = next_ps()
                        mm_group(pg[:, 0:T], [(g_t[:, k, o4 * 128:(o4 + 1) * 128], h[:, k, 128:128 + T]) for k in range(NCH)], [g_b, h_b], [pg_b])
                        gi = (o4 + b) % 2
                        K.op("act", lambda pg=pg, b=b, c=c, gi=gi: A.activation(
                            out=gtm[gi][:], in_=pg[:, 0:T], func=AF.Sigmoid, bias=bgate[:, b * 16 + c:b * 16 + c + 1]),
                            [pg_b, cb], [gtm_b[gi]])
                        if b == 0:
                            K.op("dve", lambda pz=pz, gi=gi, o4=o4: V.tensor_tensor(out=macc[o4][:], in0=gtm[gi][:], in1=pz[:, 0:T], op=ALU.mult),
                                 [gtm_b[gi], pz_b], [macc_b[o4]])
                        else:
                            K.op("dve", lambda pz=pz, gi=gi: V.tensor_tensor(out=gtm[gi][:], in0=gtm[gi][:], in1=pz[:, 0:T], op=ALU.mult),
                                 [gtm_b[gi], pz_b], [gtm_b[gi]])
                            if b == 1:
                                K.op("pool", lambda gi=gi, o4=o4: G.tensor_tensor(out=macc[o4][:], in0=macc[o4][:], in1=gtm[gi][:], op=ALU.add),
                                     [macc_b[o4], gtm_b[gi]], [macc_b[o4]])
                            else:
                                K.op("dve", lambda gi=gi, o4=o4, c=c: V.tensor_tensor(out=regA[:, 48 + c, :], in0=macc[o4][:], in1=gtm[gi][:], op=ALU.add),
                                     [macc_b[o4], gtm_b[gi]], [rA_b[48 + c]])

            def comp_out(cbk, st, sbf):
                for tb in range(TB):
                    pt, pb = next_ps()
                    mm_group(pt[:, :], [(regA[:, 48 + k, tb * 128:(tb + 1) * 128], st[:, k, :]) for k in range(NCH)],
                             [sbf] + [rA_b[48 + k] for k in range(NCH)], [pb])
                    mi = (cbk * TB + tb) % 2
                    K.op("act", lambda pt=pt, mi=mi: A.copy(out=mo_t[mi][:], in_=pt[:, :]), [pb], [mo_tb[mi]])
                    K.dma("sp", mosem[mi], mo_s[t0 + tb * 128:t0 + (tb + 1) * 128, cbk * 512:(cbk + 1) * 512], mo_t[mi][:],
                          reads=[mo_tb[mi]], writes=[mo_b[j][tb]])
            run_slabs([(w_out, 0, g * 512) for g in range(4)], comp_out)
            if DBG and j == 0:
                K.barrier()
                K.dma("sp", mosem[0], dbg_y, regA[:], reads=rA_b, writes=[])
                K.dma("sp", mosem[0], dbg_h, h[:], reads=[h_b], writes=[])
                K.barrier()
        K.barrier()

    mid.close()
    TF = 512 if L % 512 == 0 else T
    TBF = TF // 128
    NTF = L // TF
    mo_fb = [mo_b[j][t] for j in range(NT) for t in range(TB)]
    x1_fb = [x1_b[j][t] for j in range(NT) for t in range(TB)]
    with ExitStack() as es:
        gam = [sbt(es, "gam%d" % i, [128, D]) for i in range(3)]
        gam_b = Buf("gam")
        gsem = K.sem("gsem")
        for i in range(3):
            K.dma("sp", gsem, gam[i][:], rows_d[i, :].partition_broadcast(128), writes=[gam_b])
        hm = sbt(es, "hm", [128, NCH, TF], BF16)
        hm_b = Buf("hm")
        u = sbt(es, "u", [128, 64, TF], BF16)
        u_b = [Buf("u%d" % i) for i in range(64)]
        dd = sbt(es, "dd", [128, TBF, D])
        dd_b = [Buf("dd%d" % i) for i in range(TBF)]
        x1sem = K.sem("x1sem")
        rl = [sbt(es, "rl%d" % i, [128, TF]) for i in range(2)]
        rl_b = [Buf("rl%d" % i) for i in range(2)]
        osem = K.sem("osem")

        for j in range(NTF):
            t0 = j * TF
            for tb in range(TBF):
                r0 = t0 + tb * 128
                blk = r0 // 128
                K.dma("sp", tok_s[0], tok_t[0][:], mo_s[r0:r0 + 128, :], reads=[mo_fb[blk]], writes=[tok_b[0]])
                K.dma("sp", tok_s[1], tok_t[1][:], xw[128 + r0:128 + r0 + 128, :], writes=[tok_b[1]])
                rstd_from(tok_t[0], tok_b[0], xs_t[0], xs_b[0])
                K.op("dve", lambda: V.scalar_tensor_tensor(out=tok_t[0][:], in0=tok_t[0][:], scalar=st_t[:, 2:3], in1=gam[0][:],
                                                           op0=ALU.mult, op1=ALU.mult), [tok_b[0], st_b, gam_b], [tok_b[0]])
                K.op("pool", lambda: G.tensor_tensor(out=tok_t[1][:], in0=tok_t[0][:], in1=tok_t[1][:], op=ALU.add),
                     [tok_b[0], tok_b[1]], [tok_b[1]])
                K.dma("sp", x1sem, x1_s[r0:r0 + 128, :], tok_t[1][:], reads=[tok_b[1]], writes=[x1_fb[blk]])
                rstd_from(tok_t[1], tok_b[1], xs_t[0], xs_b[0])
                K.op("dve", lambda: V.scalar_tensor_tensor(out=xs_t[1][:], in0=tok_t[1][:], scalar=st_t[:, 2:3], in1=gam[1][:],
                                                           op0=ALU.mult, op1=ALU.mult), [tok_b[1], st_b, gam_b], [xs_b[1]])
                to_featmajor(xs_t[1], xs_b[1], hm, hm_b, tb * 128, None)

            def comp_up(idx, st, sbf):
                for o4 in range(4):
                    c = idx * 4 + o4
                    pt, pb = next_ps()
                    mm_group(pt[:, 0:TF], [(st[:, k, o4 * 128:(o4 + 1) * 128], hm[:, k, :]) for k in range(NCH)], [sbf, hm_b], [pb])
                    ri = c % 2
                    K.op("act", lambda pt=pt, ri=ri: A.activation(out=rl[ri][:], in_=pt[:, 0:TF], func=AF.Relu), [pb], [rl_b[ri]])
                    K.op("pool", lambda ri=ri, c=c: G.tensor_tensor(out=u[:, c, :], in0=rl[ri][:], in1=rl[ri][:], op=ALU.mult),
                         [rl_b[ri]], [u_b[c]])
            run_slabs([(w_up, 0, g * 512) for g in range(16)], comp_up)

            acc_ps = {}

            def comp_down(idx, st, sbf):
                cbk, kq = idx // 4, idx % 4
                for tb in range(TBF):
                    if kq == 0:
                        acc_ps[tb] = next_ps()
                    pt, pb = acc_ps[tb]

                    def fn(pt=pt, tb=tb, kq=kq):
                        ins = None
                        for k in range(NCH):
                            ins = nc.tensor.matmul(pt[:, :], lhsT=u[:, kq * 16 + k, tb * 128:(tb + 1) * 128], rhs=st[:, k, :],
                                                   start=(kq == 0 and k == 0), stop=(kq == 3 and k == NCH - 1))
                        return ins
                    K.op("pe", fn, [sbf] + [u_b[kq * 16 + k] for k in range(NCH)], [pb])
                    if kq == 3:
                        K.op("act", lambda pt=pt, tb=tb, cbk=cbk: A.copy(out=dd[:, tb, cbk * 512:(cbk + 1) * 512], in_=pt[:, :]),
                             [pb], [dd_b[tb]])
            run_slabs([(w_down, kq * D, cbk * 512) for cbk in range(4) for kq in range(4)], comp_down)

            for tb in range(TBF):
                r0 = t0 + tb * 128
                blk = r0 // 128
                K.dma("sp", tok_s[1], tok_t[1][:], x1_s[r0:r0 + 128, :], reads=[x1_fb[blk]], writes=[tok_b[1]])
                rstd_from(dd[:, tb, :], dd_b[tb], xs_t[0], xs_b[0])
                K.op("dve", lambda tb=tb: V.scalar_tensor_tensor(out=tok_t[0][:], in0=dd[:, tb, :], scalar=st_t[:, 2:3], in1=gam[2][:],
                                                                 op0=ALU.mult, op1=ALU.mult), [dd_b[tb], st_b, gam_b], [tok_b[0]])
                K.op("pool", lambda: G.tensor_tensor(out=tok_t[0][:], in0=tok_t[0][:], in1=tok_t[1][:], op=ALU.add),
                     [tok_b[0], tok_b[1]], [tok_b[0]])
                K.dma("sp", osem, out_d[r0:r0 + 128, :], tok_t[0][:], reads=[tok_b[0]], writes=[])
        K.barrier()
        if DBG:
            K.dma("sp", osem, dbg_x1, x1_s, reads=[], writes=[])
            K.dma("sp", osem, dbg_mo, mo_s, reads=[], writes=[])
            K.dma("sp", osem, dbg_hf, hf_s, reads=[], writes=[])
            K.barrier()
    top.close()
    return nc


_NC_CACHE = {}


def _chan(v):
    v = np.asarray(v, np.float32)
    lead = v.shape[:-1]
    r = v.reshape(lead + (16, 128))
    return np.ascontiguousarray(np.moveaxis(r, -1, 0))


def kernel(x, mem, positions, norm_mix_pre, norm_mix_post, norm_mem, w_in, b_gate, conv_w, conv_b,
           wr_f, br_f, wi_f, bi_f, lam_f, wr_b, br_b, wi_b, bi_b, lam_b, attn_sink, w_mem_kv,
           w_br_lru, w_br_attn, w_br_mem, w_out, norm_mlp_pre, norm_mlp_post, w_up, w_down):
    x = np.asarray(x, np.float32)
    B, S, _ = x.shape
    L = S // 4
    NT = L // T
    LW = L + 256
    if NT not in _NC_CACHE:
        _NC_CACHE[NT] = build(NT)
    nc = _NC_CACHE[NT]
    f = lambda a: np.ascontiguousarray(np.asarray(a, np.float32))
    xpad = np.zeros((B, S + 256, D), np.float32)
    xpad[:, 128:128 + S] = x
    pos = np.asarray(positions, np.int32)
    pospad = np.zeros((B, S + 256), np.int32)
    pospad[:, 128:128 + S] = pos
    cw = f(conv_w)[0]
    z = np.zeros_like(cw[0])
    tap_f = np.stack([z, cw[0], cw[1], cw[2], cw[3]], -1)
    tap_r = np.stack([cw[3], cw[2], cw[1], cw[0], z], -1)
    dirp = {
        "f": (f(wr_f)[0], f(wi_f)[0], f(br_f)[0].reshape(-1), f(bi_f)[0].reshape(-1), f(lam_f)[0]),
        "b": (f(wr_b)[0], f(wi_b)[0], f(br_b)[0].reshape(-1), f(bi_b)[0].reshape(-1), f(lam_b)[0]),
    }
    iu = np.arange(128)
    maskP = (iu[:, None] >= iu[None, :]).astype(np.float32)
    maskN = (iu[:, None] <= iu[None, :]).astype(np.float32)
    zero = np.zeros_like(maskP)
    half = 64
    fr = (10000.0 ** (-(np.arange(half, dtype=np.float32)) / half)).astype(np.float32)
    freq = np.stack([np.concatenate([fr, fr]), np.concatenate([fr, -fr])], -1).astype(np.float32)
    common = {
        "w_in": f(w_in)[0], "w_mem_kv": f(w_mem_kv)[0], "w_br_lru": f(w_br_lru)[0], "w_br_attn": f(w_br_attn)[0],
        "w_br_mem": f(w_br_mem)[0], "w_out": f(w_out)[0], "w_up": f(w_up)[0], "w_down": f(w_down)[0],
        "cbias": _chan(f(conv_b)[0]), "nmp": _chan(f(norm_mix_pre)[0]), "nmem": _chan(f(norm_mem)[0]),
        "bgate": np.ascontiguousarray(f(b_gate)[0].reshape(48, 128).T),
        "rows": np.ascontiguousarray(np.stack([f(norm_mix_post)[0], f(norm_mlp_pre)[0], f(norm_mlp_post)[0]], 0)),
        "sinkb": np.ascontiguousarray(np.broadcast_to(f(attn_sink)[0][None, :], (128, 16))),
        "freq": freq, "ident": np.eye(128, dtype=np.float32),
    }
    in_maps = []
    for c in range(8):
        b, q = c // 4, c % 4
        others = [("f", qq) for qq in range(q)] + [("b", qq) for qq in range(3, q, -1)]
        slots = [o[0] for o in others] + ["f", "b"]
        gw = np.stack([np.stack([np.transpose(dirp[s][0], (1, 0, 2)), np.transpose(dirp[s][1], (1, 0, 2))], 0) for s in slots], 0)
        gv = np.stack([np.stack([_chan(dirp[s][2]), _chan(dirp[s][3]), _chan(dirp[s][4])], 1) for s in slots], 1)
        taps = [tap_f if o[0] == "f" else tap_r for o in others] + [tap_f]
        ctap = np.stack([np.moveaxis(t_.reshape(16, 128, 5), 1, 0) for t_ in taps], 1)
        sel = np.zeros((128, 3, 2), np.float32)
        xo = np.zeros((3, LW, D), np.float32)
        for si, (dr, qq) in enumerate(others):
            win = xpad[b, qq * L:qq * L + LW]
            xo[si] = win if dr == "f" else win[::-1]
            sel[:, si, 0 if dr == "f" else 1] = 1.0
        first = (q == 0)
        last = (q == 3)
        masks = np.stack([maskP, maskN, zero if first else maskP, zero if last else maskN], 0)
        masks = np.ascontiguousarray(np.broadcast_to(masks[:, :, None, :], (4, 128, 4, 128)).transpose(1, 0, 2, 3))
        m = dict(common)
        m.update({
            "xw": np.ascontiguousarray(xpad[b, q * L:q * L + LW]), "xo": xo,
            "posw": np.ascontiguousarray(pospad[b:b + 1, q * L:q * L + LW]),
            "memb": f(mem)[b], "gw": np.ascontiguousarray(gw.astype(np.float32)), "gv": np.ascontiguousarray(gv),
            "ctap": np.ascontiguousarray(ctap), "sel": sel, "masks": masks,
        })
        in_maps.append(m)
    res = run_bass_kernel_spmd(nc, in_maps, core_ids=list(range(8)))
    global LAST_RES
    LAST_RES = res
    out = np.zeros((B, S, D), np.float32)
    for c in range(8):
        b, q = c // 4, c % 4
        out[b, q * L:(q + 1) * L] = res.results[c]["out"]
    return out
```

```python
import os
import numpy as np
from contextlib import ExitStack
import concourse.bass as bass
import concourse.mybir as mybir
from concourse.bass_utils import run_bass_kernel_spmd

F32 = mybir.dt.float32
BF16 = mybir.dt.bfloat16
I32 = mybir.dt.int32
ALU = mybir.AluOpType
AF = mybir.ActivationFunctionType

D = 2048
NCH = 16
T = 256
TB = T // 128
WIN = T + 256
NBLK = WIN // 128
DFF = 8192
NIN = 15360
EPS = 1e-6
NSLAB = 2


class Sem:
    def __init__(self, nc, name, reg):
        self.h = nc.alloc_semaphore(name)
        self.v = 0
        reg.append(self)


class Buf:
    __slots__ = ("name", "w", "r")

    def __init__(self, name):
        self.name = name
        self.w = None
        self.r = {}


class Ctx:
    def __init__(self, nc):
        self.nc = nc
        self.sems = []
        self.eng = {"pe": nc.tensor, "act": nc.scalar, "dve": nc.vector, "pool": nc.gpsimd, "sp": nc.sync}
        self.esem = {k: Sem(nc, "s_" + k, self.sems) for k in ("pe", "act", "dve", "pool")}
        self.seen = {k: {} for k in self.eng}

    def sem(self, name):
        return Sem(self.nc, name, self.sems)

    def _deps(self, reads, writes):
        d = {}

        def add(s, v):
            if d.get(s, 0) < v:
                d[s] = v
        for b in reads:
            if b.w is not None:
                add(*b.w)
        for b in writes:
            if b.w is not None:
                add(*b.w)
            for s, v in b.r.items():
                add(s, v)
        return d

    def _wait(self, e, d):
        seen = self.seen[e]
        for s, v in d.items():
            if seen.get(s, 0) < v:
                self.eng[e].wait_ge(s.h, v)
                seen[s] = v

    def _commit(self, ev, reads, writes):
        s, v = ev
        for b in writes:
            b.w = ev
            b.r = {}
        for b in reads:
            if b.r.get(s, 0) < v:
                b.r[s] = v

    def op(self, e, fn, reads=(), writes=()):
        self._wait(e, self._deps(reads, writes))
        ins = fn()
        s = self.esem[e]
        s.v += 1
        ins.then_inc(s.h, 1)
        self._commit((s, s.v), reads, writes)

    def dma(self, e, sem, out, in_, reads=(), writes=()):
        self._wait(e, self._deps(reads, writes))
        ins = self.eng[e].dma_start(out=out, in_=in_)
        sem.v += 16
        ins.then_inc(sem.h, 16)
        self._commit((sem, sem.v), reads, writes)

    def barrier(self):
        for e in self.eng:
            d = {s: s.v for s in self.sems if s.v > 0}
            self._wait(e, d)


def build(NT):
    L = NT * T
    LW = L + 256
    nc = bass.Bass("TRN2", target_bir_lowering=False)
    K = Ctx(nc)

    def din(name, shape, dt=F32):
        return nc.dram_tensor(name, shape, dt, kind="ExternalInput").ap()

    def dint(name, shape, dt=F32):
        return nc.dram_tensor(name, shape, dt, kind="Internal").ap()

    xw = din("xw", [LW, D])
    xo = din("xo", [3, LW, D])
    posw = din("posw", [1, LW], I32)
    memb = din("memb", [256, D])
    w_in = din("w_in", [D, NIN])
    w_mem_kv = din("w_mem_kv", [D, 2 * D])
    w_br = [din("w_br_lru", [D, D]), din("w_br_attn", [D, D]), din("w_br_mem", [D, D])]
    w_out = din("w_out", [D, D])
    w_up = din("w_up", [D, DFF])
    w_down = din("w_down", [DFF, D])
    gw_d = din("gw", [5, 2, 128, NCH, 128])
    gv_d = din("gv", [128, 5, 3, NCH])
    ctap_d = din("ctap", [128, 4, NCH, 5])
    cbias_d = din("cbias", [128, NCH])
    sel_d = din("sel", [128, 3, 2])
    nmp_d = din("nmp", [128, NCH])
    nmem_d = din("nmem", [128, NCH])
    bgate_d = din("bgate", [128, 48])
    rows_d = din("rows", [3, D])
    sink_d = din("sinkb", [128, NCH])
    freq_d = din("freq", [128, 2])
    masks_d = din("masks", [128, 4, 4, 128])
    ident_d = din("ident", [128, 128])
    out_d = nc.dram_tensor("out", [L, D], F32, kind="ExternalOutput").ap()

    DBG = bool(int(os.environ.get("KDBG", "0")))
    if DBG:
        dbg_x1 = nc.dram_tensor("dbg_x1", [L, D], F32, kind="ExternalOutput").ap()
        dbg_mo = nc.dram_tensor("dbg_mo", [L, D], F32, kind="ExternalOutput").ap()
        dbg_hf = nc.dram_tensor("dbg_hf", [NT, 128, NCH, T], F32, kind="ExternalOutput").ap()
        dbg_y = nc.dram_tensor("dbg_y", [128, 64, T], BF16, kind="ExternalOutput").ap()
        dbg_h = nc.dram_tensor("dbg_h", [128, NCH, WIN], BF16, kind="ExternalOutput").ap()
    hf_s = dint("hf_s", [NT, 128, NCH, T])
    ab_s = dint("ab_s", [NT, 128, NCH, T])
    ub_s = dint("ub_s", [NT, 128, NCH, T])
    mo_s = dint("mo_s", [L, D])
    x1_s = dint("x1_s", [L, D])
    hf_b = [[Buf("hf%d_%d" % (j, g)) for g in range(4)] for j in range(NT)]
    ab_b = [[Buf("ab%d_%d" % (j, g)) for g in range(4)] for j in range(NT)]
    ub_b = [[Buf("ub%d_%d" % (j, g)) for g in range(4)] for j in range(NT)]
    mo_b = [[Buf("mo%d_%d" % (j, t)) for t in range(TB)] for j in range(NT)]
    x1_b = [[Buf("x1%d_%d" % (j, t)) for t in range(TB)] for j in range(NT)]

    top = ExitStack()

    def sbt(es, name, shape, dt=F32):
        return es.enter_context(nc.sbuf_tensor("sb_" + name, shape, dt))

    NPS = 5
    ps_t = [top.enter_context(nc.psum_tensor("ps%d" % i, [128, 512], F32)) for i in range(NPS)]
    ps_b = [Buf("ps%d" % i) for i in range(NPS)]
    pst_t = [top.enter_context(nc.psum_tensor("pst%d" % i, [128, 1024], BF16)) for i in range(2)]
    pst_b = [Buf("pst%d" % i) for i in range(2)]
    psh_t = top.enter_context(nc.psum_tensor("psh", [128, 512], F32))
    psh_b = Buf("psh")
    ps_rr = [0]

    def next_ps():
        i = ps_rr[0] % NPS
        ps_rr[0] += 1
        return ps_t[i], ps_b[i]

    def mm_group(out_ap, pairs, reads, writes):
        def fn():
            n = len(pairs)
            ins = None
            for i, (l, r) in enumerate(pairs):
                ins = nc.tensor.matmul(out_ap, lhsT=l, rhs=r, start=(i == 0), stop=(i == n - 1))
            return ins
        K.op("pe", fn, reads, writes)

    csem = K.sem("csem")
    cb = Buf("consts")
    ident_b = sbt(top, "ident_b", [128, 128], BF16)
    ones_b = sbt(top, "ones_b", [128, 128], BF16)
    masks_b = sbt(top, "masks_b", [128, 4, 4, 128], BF16)
    gv = sbt(top, "gv", [128, 5, 3, NCH])
    gvh = sbt(top, "gvh", [128, 5, 3, NCH])
    cl = sbt(top, "cl", [128, 5, NCH])
    clh = sbt(top, "clh", [128, 5, NCH])
    clq = sbt(top, "clq", [128, 5, NCH])
    ctmp = sbt(top, "ctmp", [128, 5, NCH])
    ctmp2 = sbt(top, "ctmp2", [128, 5, NCH])
    ctap = sbt(top, "ctap", [128, 4, NCH, 5])
    cbias = sbt(top, "cbias", [128, NCH])
    sel = sbt(top, "sel", [128, 3, 2])
    nmp = sbt(top, "nmp", [128, NCH])
    nmem = sbt(top, "nmem", [128, NCH])
    bgate = sbt(top, "bgate", [128, 48])
    esink = sbt(top, "esink", [128, NCH])
    freq = sbt(top, "freq", [128, 2])
    neghalf = sbt(top, "neghalf", [128, 8])
    carry = sbt(top, "carry", [128, 2, NCH])
    mk_b = Buf("mkT")
    mv_b = Buf("mv")
    carry_b = Buf("carry")

    for dst, src in ((gv, gv_d), (ctap, ctap_d), (cbias, cbias_d),
                     (sel, sel_d), (nmp, nmp_d), (nmem, nmem_d), (bgate, bgate_d), (esink, sink_d), (freq, freq_d)):
        K.dma("sp", csem, dst[:], src, writes=[cb])
    V = nc.vector
    A = nc.scalar
    G = nc.gpsimd
    K.op("dve", lambda: V.memset(ones_b[:], 1.0), [], [cb])
    K.op("dve", lambda: V.memset(neghalf[:], -0.5), [], [cb])
    K.op("dve", lambda: V.memset(carry[:], 0.0), [], [carry_b])
    K.op("dve", lambda: V.tensor_scalar(out=gvh[:], in0=gv[:], scalar1=0.5, scalar2=None, op0=ALU.mult), [cb], [cb])
    lam = gv[:, :, 2, :]
    K.op("act", lambda: A.activation(out=ctmp[:], in_=lam, func=AF.Abs), [cb], [cb])
    K.op("act", lambda: A.activation(out=ctmp[:], in_=ctmp[:], func=AF.Exp, scale=-1.0), [cb], [cb])
    K.op("act", lambda: A.activation(out=ctmp[:], in_=ctmp[:], func=AF.Ln, bias=1.0), [cb], [cb])
    K.op("dve", lambda: V.tensor_scalar(out=ctmp2[:], in0=lam, scalar1=-1.0, scalar2=0.0, op0=ALU.mult, op1=ALU.max), [cb], [cb])
    K.op("dve", lambda: V.tensor_tensor(out=ctmp[:], in0=ctmp[:], in1=ctmp2[:], op=ALU.add), [cb], [cb])
    K.op("dve", lambda: V.tensor_scalar(out=cl[:], in0=ctmp[:], scalar1=-8.0, scalar2=None, op0=ALU.mult), [cb], [cb])
    K.op("dve", lambda: V.tensor_scalar(out=clh[:], in0=ctmp[:], scalar1=-4.0, scalar2=None, op0=ALU.mult), [cb], [cb])
    K.op("dve", lambda: V.tensor_scalar(out=clq[:], in0=ctmp[:], scalar1=-2.0, scalar2=None, op0=ALU.mult), [cb], [cb])
    K.op("act", lambda: A.activation(out=esink[:], in_=esink[:], func=AF.Exp), [cb], [cb])

    slab_t = [sbt(top, "slab%d" % i, [128, NCH, 512], BF16) for i in range(NSLAB)]
    slab_b = [Buf("slab%d" % i) for i in range(NSLAB)]
    slab_s = [K.sem("slabsem%d" % i) for i in range(NSLAB)]
    slab_rr = [0]

    pc_specs = [(w_mem_kv, 0, g * 512) for g in range(8)] + [(w_in, 0, g * 512) for g in range(4)]
    pc_specs += [(w_in, 0, 6144), (w_in, 0, 6656)] + [(w_in, 0, 4096 + g * 512) for g in range(4)]
    pc_specs += [(w_in, 0, 7168 + g * 512) for g in range(4)] + [(w_in, 0, 2048 + g * 512) for g in range(4)]
    for cbk_ in range(4):
        for b_ in range(3):
            pc_specs += [(w_br[b_], 0, cbk_ * 512), (w_in, 0, 9216 + b_ * D + cbk_ * 512)]
    pc_specs += [(w_out, 0, g * 512) for g in range(4)] + [(w_up, 0, g * 512) for g in range(16)]
    pc_specs += [(w_down, kq * D, cbk_ * 512) for cbk_ in range(4) for kq in range(4)]
    NSL = len(pc_specs)
    wsc = dint("wsc", [NSL, 128, NCH, 512], BF16)
    wsc_b = [Buf("wsc%d" % i) for i in range(NSL)]
    pcsem = K.sem("pcsem")
    pc_id = {}
    for i_, (Wap_, r0_, c0_) in enumerate(pc_specs):
        pc_id[(id(Wap_), r0_, c0_)] = i_
        K.dma("pool", pcsem, wsc[i_], Wap_[r0_:r0_ + D, c0_:c0_ + 512].rearrange("(c p) j -> p c j", p=128),
              writes=[wsc_b[i_]])

    def load_slab_into(t_, b_, s_, Wap, r0, c0):
        i_ = pc_id[(id(Wap), r0, c0)]
        K.dma("sp", s_, t_[:], wsc[i_], reads=[wsc_b[i_]], writes=[b_])

    def load_slab(Wap, r0, c0, ncols=512):
        i = slab_rr[0] % NSLAB
        slab_rr[0] += 1
        load_slab_into(slab_t[i], slab_b[i], slab_s[i], Wap, r0, c0)
        return slab_t[i], slab_b[i]

    def run_slabs(specs, compute, pre=NSLAB - 1):
        pend = []
        for idx, sp in enumerate(specs):
            pend.append((idx, sp, load_slab(*sp)))
            if len(pend) > pre:
                i0, s0, (t0, b0) = pend.pop(0)
                compute(i0, t0, b0)
        for i0, s0, (t0, b0) in pend:
            compute(i0, t0, b0)

    tok_t = [sbt(top, "tok%d" % i, [128, D]) for i in range(2)]
    tok_b = [Buf("tok%d" % i) for i in range(2)]
    tok_s = [K.sem("toksem%d" % i) for i in range(2)]
    xs_t = [sbt(top, "xs%d" % i, [128, D], BF16) for i in range(2)]
    xs_b = [Buf("xs%d" % i) for i in range(2)]
    st_t = sbt(top, "stats", [128, 8])
    st_b = Buf("stats")
    tok_rr = [0]

    def rstd_from(src_t, src_b, junk_t, junk_b):
        K.op("act", lambda: A.activation(out=junk_t[:], in_=src_t[:], func=AF.Square, accum_out=st_t[:, 0:1]),
             [src_b], [junk_b, st_b])
        K.op("dve", lambda: V.tensor_scalar(out=st_t[:, 1:2], in0=st_t[:, 0:1], scalar1=1.0 / D, scalar2=EPS,
                                            op0=ALU.mult, op1=ALU.add), [st_b], [st_b])
        K.op("act", lambda: A.activation(out=st_t[:, 3:4], in_=st_t[:, 1:2], func=AF.Sqrt), [st_b], [st_b])
        K.op("dve", lambda: V.reciprocal(out=st_t[:, 2:3], in_=st_t[:, 3:4]), [st_b], [st_b])

    def to_featmajor(xsrc_t, xsrc_b, dst_t, dst_b, col0, gam, nrows=128):
        for half in range(2):
            pt, pb = pst_t[half], pst_b[half]

            def fn(half=half, pt=pt):
                ins = None
                for c8 in range(8):
                    c = half * 8 + c8
                    ins = nc.tensor.transpose(out=pt[:, c8 * 128:(c8 + 1) * 128], in_=xsrc_t[:, c * 128:(c + 1) * 128],
                                              identity=ident_b[:])
                return ins
            K.op("pe", fn, [xsrc_b, cb], [pb])
            if gam is None:
                K.op("act", lambda half=half, pt=pt: A.copy(
                    out=dst_t[:, half * 8:(half + 1) * 8, col0:col0 + 128],
                    in_=pt[:].rearrange("p (c t) -> p c t", c=8)), [pb], [dst_b])
            else:
                for c8 in range(8):
                    c = half * 8 + c8
                    e = "dve"
                    if e == "act":
                        K.op("act", lambda c=c, c8=c8, pt=pt: A.activation(
                            out=dst_t[:, c, col0:col0 + 128], in_=pt[:, c8 * 128:(c8 + 1) * 128],
                            func=AF.Copy, scale=gam[:, c:c + 1]), [pb, cb], [dst_b])
                    else:
                        K.op("dve", lambda c=c, c8=c8, pt=pt: V.tensor_scalar(
                            out=dst_t[:, c, col0:col0 + 128], in0=pt[:, c8 * 128:(c8 + 1) * 128],
                            scalar1=gam[:, c:c + 1], scalar2=None, op0=ALU.mult), [pb, cb], [dst_b])

    def make_h(src_rows, nblk, dst_t, dst_b, gam):
        for blk in range(nblk):
            i = tok_rr[0] % 2
            tok_rr[0] += 1
            K.dma("sp", tok_s[i], tok_t[i][:], src_rows[blk * 128:(blk + 1) * 128, :], writes=[tok_b[i]])
            rstd_from(tok_t[i], tok_b[i], xs_t[i], xs_b[i])
            K.op("dve", lambda i=i: V.tensor_scalar(out=xs_t[i][:], in0=tok_t[i][:], scalar1=st_t[:, 2:3], scalar2=None,
                                                    op0=ALU.mult), [tok_b[i], st_b], [xs_b[i]])
            to_featmajor(xs_t[i], xs_b[i], dst_t, dst_b, blk * 128, gam)

    with ExitStack() as ctemp:
        ident_f = sbt(ctemp, "ident_f", [128, 128])
        masks_f = sbt(ctemp, "masks_f", [128, 4, 4, 128])
        K.dma("sp", csem, ident_f[:], ident_d, writes=[cb])
        K.dma("sp", csem, masks_f[:], masks_d, writes=[cb])
        K.op("dve", lambda: V.tensor_copy(out=ident_b[:], in_=ident_f[:]), [cb], [cb])
        K.op("dve", lambda: V.tensor_copy(out=masks_b[:], in_=masks_f[:]), [cb], [cb])
        K.barrier()
    mid = ExitStack()
    mkT = sbt(mid, "mkT", [128, NCH, 256], BF16)
    mv = sbt(mid, "mv", [128, 2, D], BF16)

    with ExitStack() as es:
        mnT = sbt(es, "mnT", [128, NCH, 256], BF16)
        mn_b = Buf("mnT")
        make_h(memb, 2, mnT, mn_b, nmem)

        def comp_mk(idx, st, sbf):
            for o4 in range(4):
                c = idx * 4 + o4
                pt, pb = next_ps()
                mm_group(pt[:, 0:256], [(st[:, k, o4 * 128:(o4 + 1) * 128], mnT[:, k, :]) for k in range(NCH)],
                         [sbf, mn_b], [pb])
                K.op("act", lambda c=c, pt=pt: A.copy(out=mkT[:, c, :], in_=pt[:, 0:256]), [pb], [mk_b])
        run_slabs([(w_mem_kv, 0, g * 512) for g in range(4)], comp_mk)

        def comp_mv(idx, st, sbf):
            for blk in range(2):
                pt, pb = next_ps()
                mm_group(pt[:, :], [(mnT[:, k, blk * 128:(blk + 1) * 128], st[:, k, :]) for k in range(NCH)],
                         [sbf, mn_b], [pb])
                K.op("act", lambda blk=blk, pt=pt, idx=idx: A.copy(out=mv[:, blk, idx * 512:(idx + 1) * 512], in_=pt[:, :]),
                     [pb], [mv_b])
        run_slabs([(w_mem_kv, 0, D + g * 512) for g in range(4)], comp_mv)
        K.barrier()

    with ExitStack() as es:
        hL = sbt(es, "hL", [128, NCH, WIN], BF16)
        hL_b = Buf("hL")
        gwt = [[sbt(es, "gw%d_%d" % (d, ri), [128, NCH, 128], BF16) for ri in range(2)] for d in range(2)]
        gw_b = Buf("gw")
        gwsem = K.sem("gwsem")
        xr_sb = sbt(es, "xr_sb", [128, 4, T + 4])
        xr_bs = [Buf("xr%d" % i) for i in range(4)]
        xc = sbt(es, "xc", [128, 4, T])
        xc_bs = [Buf("xc%d" % i) for i in range(4)]
        xcb = sbt(es, "xcb", [128, 4, T], BF16)
        xcb_bs = [Buf("xcb%d" % i) for i in range(4)]
        Rt = [sbt(es, "R%d" % d, [128, 4, T]) for d in range(2)]
        It = [sbt(es, "I%d" % d, [128, 4, T]) for d in range(2)]
        At = [sbt(es, "A%d" % d, [128, 4, T]) for d in range(2)]
        R_bs = [[Buf("R%d_%d" % (d, i)) for i in range(4)] for d in range(2)]
        I_bs = [[Buf("I%d_%d" % (d, i)) for i in range(4)] for d in range(2)]
        A_bs = [[Buf("A%d_%d" % (d, i)) for i in range(4)] for d in range(2)]
        Wt = [sbt(es, "Wt%d" % d, [128, 4, T]) for d in range(2)]
        W_b = [Buf("Wt%d" % d) for d in range(2)]
        HF = sbt(es, "HF", [128, 4, T])
        HF_bs = [Buf("HF%d" % i) for i in range(4)]
        spsem = K.sem("spsem")
        st8 = sbt(es, "st8", [128, NCH])
        st8_bs = [Buf("st8_%d" % i) for i in range(NCH)]
        rsum = sbt(es, "rsum", [128, NCH])
        rs_t = sbt(es, "rs_t", [128, NCH])
        rs_b = Buf("rsum")
        rs_bs = [Buf("rs_%d" % i) for i in range(NCH)]
        aend = sbt(es, "aend", [128, NCH])

        xsl_t = [slab_t[0], slab_t[1], sbt(es, "xsl2", [128, NCH, 512], BF16), sbt(es, "xsl3", [128, NCH, 512], BF16)]
        xsl_b = [slab_b[0], slab_b[1], Buf("xsl2"), Buf("xsl3")]
        xsl_s = [slab_s[0], slab_s[1], K.sem("xslsem2"), K.sem("xslsem3")]
        for g_ in range(4):
            load_slab_into(xsl_t[g_], xsl_b[g_], xsl_s[g_], w_in, 0, g_ * 512)

        def load_gw(slots):
            for d, sl in enumerate(slots):
                for ri in range(2):
                    K.dma("pool", gwsem, gwt[d][ri][:], gw_d[sl, ri], writes=[gw_b])

        def lru_tile(src_rows, slots, tapi, spill_j):
            nd = len(slots)
            make_h(src_rows, NBLK, hL, hL_b, nmp)

            def comp(og, st, sbf):
                psB, psB_b = psh_t, psh_b
                pAs = []
                for o4 in range(4):
                    pA, pA_b = next_ps()
                    pAs.append((pA, pA_b))
                    lw = [st[:, k, o4 * 128:(o4 + 1) * 128] for k in range(NCH)]
                    mm_group(pA[:, 0:T], [(lw[k], hL[:, k, 126:126 + T]) for k in range(NCH)], [sbf, hL_b], [pA_b])
                    mm_group(psB[:, o4 * 4:o4 * 4 + 4], [(lw[k], hL[:, k, 126 + T:130 + T]) for k in range(NCH)], [sbf, hL_b], [psB_b])
                for o4 in range(4):
                    pA, pA_b = pAs[o4]
                    K.op("act", lambda o4=o4, pA=pA: A.copy(out=xr_sb[:, o4, 0:T], in_=pA[:, 0:T]), [pA_b], [xr_bs[o4]])
                    K.op("act", lambda o4=o4: A.copy(out=xr_sb[:, o4, T:T + 4], in_=psB[:, o4 * 4:o4 * 4 + 4]), [psB_b], [xr_bs[o4]])
                for j in range(5):
                    for o4 in range(4):
                        c = og * 4 + o4
                        if j == 0:
                            K.op("dve", lambda o4=o4, c=c: V.tensor_scalar(
                                out=xc[:, o4, :], in0=xr_sb[:, o4, 0:T], scalar1=ctap[:, tapi, c, 0:1], scalar2=cbias[:, c:c + 1],
                                op0=ALU.mult, op1=ALU.add), [xr_bs[o4], cb], [xc_bs[o4]])
                        else:
                            K.op("dve", lambda o4=o4, c=c, j=j: V.scalar_tensor_tensor(
                                out=xc[:, o4, :], in0=xr_sb[:, o4, j:j + T], scalar=ctap[:, tapi, c, j:j + 1], in1=xc[:, o4, :],
                                op0=ALU.mult, op1=ALU.add), [xr_bs[o4], cb, xc_bs[o4]], [xc_bs[o4]])
                for o4 in range(4):
                    K.op("pool", lambda o4=o4: G.tensor_copy(out=xcb[:, o4, :], in_=xc[:, o4, :]), [xc_bs[o4]], [xcb_bs[o4]])
                for o4 in range(4):
                    c = og * 4 + o4
                    for d in range(nd):
                        sl = slots[d]
                        pr, pr_b = next_ps()
                        mm_group(pr[:, 0:T], [(gwt[d][0][:, c, :], xcb[:, o4, :])], [gw_b, xcb_bs[o4]], [pr_b])
                        acc = rs_t[:, c:c + 1] if d == 0 else None
                        K.op("act", lambda d=d, o4=o4, c=c, pr=pr, sl=sl, acc=acc: A.activation(
                            out=Rt[d][:, o4, :], in_=pr[:, 0:T], func=AF.Tanh, bias=gvh[:, sl, 0, c:c + 1], scale=0.5,
                            accum_out=acc), [pr_b, cb], [R_bs[d][o4]] + ([rs_bs[c]] if d == 0 else []))
                        pi, pi_b = next_ps()
                        mm_group(pi[:, 0:T], [(gwt[d][1][:, c, :], xcb[:, o4, :])], [gw_b, xcb_bs[o4]], [pi_b])
                        K.op("act", lambda d=d, o4=o4, c=c, pi=pi, sl=sl: A.activation(
                            out=It[d][:, o4, :], in_=pi[:, 0:T], func=AF.Tanh, bias=gvh[:, sl, 1, c:c + 1], scale=0.5),
                            [pi_b, cb], [I_bs[d][o4]])
                for d in range(nd):
                    sl = slots[d]
                    for o4 in range(4):
                        c = og * 4 + o4
                        K.op("act", lambda d=d, o4=o4, c=c, sl=sl: A.activation(
                            out=At[d][:, o4, :], in_=Rt[d][:, o4, :], func=AF.Tanh, scale=clq[:, sl, c:c + 1],
                            bias=clq[:, sl, c:c + 1]), [R_bs[d][o4], cb], [A_bs[d][o4]])
                for d in range(nd):
                    K.op("act", lambda d=d: A.activation(out=Rt[d][:], in_=At[d][:], func=AF.Sqrt, scale=-1.0),
                         A_bs[d], R_bs[d])
                    K.op("dve", lambda d=d: V.tensor_scalar(out=Wt[d][:], in0=At[d][:], scalar1=-1.0, scalar2=1.0, op0=ALU.mult, op1=ALU.add),
                         A_bs[d], [W_b[d]])
                    K.op("dve", lambda d=d: V.reciprocal(out=Wt[d][:], in_=Wt[d][:]), [W_b[d]], [W_b[d]])
                    K.op("dve", lambda d=d: V.scalar_tensor_tensor(out=At[d][:], in0=At[d][:], scalar=1.0, in1=Wt[d][:],
                                                                   op0=ALU.add, op1=ALU.mult), A_bs[d] + [W_b[d]], A_bs[d])
                    K.op("dve", lambda d=d: V.scalar_tensor_tensor(out=It[d][:], in0=It[d][:], scalar=1.0, in1=xc[:],
                                                                   op0=ALU.add, op1=ALU.mult), I_bs[d] + xc_bs, I_bs[d])
                    K.op("pool", lambda d=d: G.tensor_tensor(out=It[d][:], in0=It[d][:], in1=Rt[d][:], op=ALU.mult),
                         I_bs[d] + R_bs[d], I_bs[d])
                    K.op("pool", lambda d=d: G.tensor_tensor(out=It[d][:], in0=It[d][:], in1=Wt[d][:], op=ALU.mult),
                         I_bs[d] + [W_b[d]], I_bs[d])
                for o4 in range(4):
                    c = og * 4 + o4
                    K.op("dve", lambda o4=o4, c=c: V.tensor_tensor_scan(
                        out=HF[:, o4, :], data0=At[0][:, o4, :], data1=It[0][:, o4, :], initial=st8[:, c:c + 1],
                        op0=ALU.mult, op1=ALU.add), [A_bs[0][o4], I_bs[0][o4], st8_bs[c]], [HF_bs[o4]])
                for o4 in range(4):
                    c = og * 4 + o4
                    K.op("pool", lambda o4=o4, c=c: G.tensor_copy(out=st8[:, c:c + 1], in_=HF[:, o4, T - 1:T]), [HF_bs[o4]], [st8_bs[c]])
                if spill_j is not None:
                    j = spill_j
                    K.dma("sp", spsem, hf_s[j, :, og * 4:(og + 1) * 4, :], HF[:], reads=HF_bs, writes=[hf_b[j][og]])
                    K.dma("sp", spsem, ab_s[j, :, og * 4:(og + 1) * 4, :], At[1][:], reads=A_bs[1], writes=[ab_b[j][og]])
                    K.dma("sp", spsem, ub_s[j, :, og * 4:(og + 1) * 4, :], It[1][:], reads=I_bs[1], writes=[ub_b[j][og]])
            for g_ in range(4):
                comp(g_, xsl_t[g_], xsl_b[g_])
            K.op("dve", lambda: V.tensor_tensor(out=rsum[:], in0=rsum[:], in1=rs_t[:], op=ALU.add), [rs_b] + rs_bs, [rs_b])

        for s in range(3):
            load_gw([s])
            K.op("dve", lambda: V.memset(st8[:], 0.0), [], st8_bs)
            K.op("dve", lambda: V.memset(rsum[:], 0.0), [], [rs_b])
            for j in range(NT):
                lru_tile(xo[s, j * T:j * T + WIN, :], [s], s, None)
            K.op("dve", lambda: V.tensor_scalar(out=rsum[:], in0=rsum[:], scalar1=0.5, scalar2=0.5 * L, op0=ALU.mult, op1=ALU.add),
                 [rs_b], [rs_b])
            K.op("dve", lambda s=s: V.tensor_tensor(out=rsum[:], in0=rsum[:], in1=cl[:, s, :], op=ALU.mult), [rs_b, cb], [rs_b])
            K.op("act", lambda: A.activation(out=aend[:], in_=rsum[:], func=AF.Exp), [rs_b], [rs_b])
            for dirn in range(2):
                K.op("dve", lambda dirn=dirn: V.tensor_tensor(out=rsum[:], in0=aend[:], in1=carry[:, dirn, :], op=ALU.mult),
                     [rs_b, carry_b], [rs_b])
                K.op("dve", lambda: V.tensor_tensor(out=rsum[:], in0=rsum[:], in1=st8[:], op=ALU.add), [rs_b] + st8_bs, [rs_b])
                K.op("dve", lambda dirn=dirn: V.tensor_tensor(out=rsum[:], in0=rsum[:], in1=carry[:, dirn, :], op=ALU.subtract),
                     [rs_b, carry_b], [rs_b])
                K.op("dve", lambda dirn=dirn, s=s: V.scalar_tensor_tensor(
                    out=carry[:, dirn, :], in0=rsum[:], scalar=sel[:, s, dirn:dirn + 1], in1=carry[:, dirn, :],
                    op0=ALU.mult, op1=ALU.add), [rs_b, carry_b, cb], [carry_b])
        load_gw([3, 4])
        K.op("dve", lambda: V.tensor_copy(out=st8[:], in_=carry[:, 0, :]), [carry_b], st8_bs)
        for j in range(NT):
            lru_tile(xw[j * T:j * T + WIN, :], [3, 4], 3, j)
        K.barrier()

    with ExitStack() as es:
        h = sbt(es, "h", [128, NCH, WIN], BF16)
        h_b = Buf("h")
        regA = sbt(es, "regA", [128, 64, T], BF16)
        rA_b = [Buf("rA%d" % i) for i in range(64)]
        kT = sbt(es, "kT", [128, 4, WIN], BF16)
        kT_b = Buf("kT")
        vtok = sbt(es, "vtok", [128, NBLK, 512], BF16)
        v_b = Buf("vtok")
        qb_t = [sbt(es, "qb%d" % i, [128, 4, T], BF16) for i in range(2)]
        qb_b = [Buf("qb%d" % i) for i in range(2)]
        cosT = sbt(es, "cosT", [128, WIN])
        nsinT = sbt(es, "nsinT", [128, WIN])
        trig_b = Buf("trig")
        posi = sbt(es, "posi", [128, WIN], I32)
        posf = sbt(es, "posf", [128, WIN])
        possem = K.sem("possem")
        rp1 = sbt(es, "rp1", [128, WIN])
        rp2 = sbt(es, "rp2", [128, WIN])
        rp_b = Buf("rp")
        tg1, tg2, tgi = rp1, rp2, posi
        pT = [sbt(es, "pT%d" % i, [128, 512], BF16) for i in range(6)]
        pT_b = [Buf("pT%d" % i) for i in range(6)]
        rden = sbt(es, "rden", [128, 512])
        rden_b = Buf("rden")
        lbuf = [[sbt(es, "lb%d_%d" % (i, k), [128, T]) for k in range(3)] for i in range(2)]
        lbuf_b = [[Buf("lb%d_%d" % (i, k)) for k in range(3)] for i in range(2)]
        lsem = [K.sem("lsem%d" % i) for i in range(2)]
        hb = sbt(es, "hb", [128, T])
        hb_b = Buf("hb")
        gel = sbt(es, "gel", [128, T])
        gel_b = Buf("gel")
        mo_t = [sbt(es, "mo%d" % i, [128, 512]) for i in range(2)]
        mo_tb = [Buf("mot%d" % i) for i in range(2)]
        mosem = [K.sem("mosem%d" % i) for i in range(2)]
        macc = [sbt(es, "macc%d" % i, [128, T]) for i in range(4)]
        macc_b = [Buf("macc%d" % i) for i in range(4)]
        gtm = [sbt(es, "gtm%d" % i, [128, T]) for i in range(2)]
        gtm_b = [Buf("gtm%d" % i) for i in range(2)]
        stb = sbt(es, "stb", [128, NCH])
        stb_b = Buf("stb")
        K.op("dve", lambda: V.tensor_copy(out=stb[:], in_=carry[:, 1, :]), [carry_b], [stb_b])

        def rope(ps, ps_b_, dst_ap, c0, n):
            K.op("dve", lambda: V.tensor_tensor(out=rp1[0:64, 0:n], in0=ps[64:128, 0:n], in1=nsinT[64:128, c0:c0 + n], op=ALU.mult),
                 [ps_b_, trig_b], [rp_b])
            K.op("dve", lambda: V.tensor_tensor(out=rp1[64:128, 0:n], in0=ps[0:64, 0:n], in1=nsinT[0:64, c0:c0 + n], op=ALU.mult),
                 [ps_b_, trig_b], [rp_b])
            K.op("dve", lambda: V.tensor_tensor(out=rp2[:, 0:n], in0=ps[:, 0:n], in1=cosT[:, c0:c0 + n], op=ALU.mult),
                 [ps_b_, trig_b], [rp_b])
            return lambda wr: K.op("pool", lambda: G.tensor_tensor(out=dst_ap, in0=rp1[:, 0:n], in1=rp2[:, 0:n], op=ALU.add),
                                   [rp_b], wr)

        for j in range(NT - 1, -1, -1):
            t0 = j * T
            make_h(xw[t0:t0 + WIN, :], NBLK, h, h_b, nmp)
            K.dma("sp", possem, posi[:], posw[0, t0:t0 + WIN].partition_broadcast(128), writes=[trig_b])
            K.op("dve", lambda: V.tensor_copy(out=posf[:], in_=posi[:]), [trig_b, rp_b], [trig_b, rp_b])
            for (dst, col, shift) in ((nsinT, 1, 0.0), (cosT, 0, 0.25)):
                K.op("dve", lambda col=col, shift=shift: V.tensor_scalar(
                    out=tg1[:], in0=posf[:], scalar1=freq[:, col:col + 1], scalar2=None, op0=ALU.mult), [trig_b, cb, rp_b], [trig_b, rp_b])
                K.op("dve", lambda shift=shift: V.tensor_scalar(
                    out=tg1[:], in0=tg1[:], scalar1=float(1.0 / (2 * np.pi)), scalar2=shift, op0=ALU.mult, op1=ALU.add),
                    [trig_b, rp_b], [trig_b, rp_b])
                K.op("dve", lambda: V.tensor_copy(out=tgi[:], in_=tg1[:]), [trig_b, rp_b], [trig_b, rp_b])
                K.op("dve", lambda: V.tensor_copy(out=tg2[:], in_=tgi[:]), [trig_b, rp_b], [trig_b, rp_b])
                K.op("dve", lambda: V.tensor_tensor(out=tg1[:], in0=tg1[:], in1=tg2[:], op=ALU.subtract), [trig_b, rp_b], [trig_b, rp_b])
                K.op("dve", lambda: V.tensor_single_scalar(out=tg2[:], in_=tg1[:], scalar=0.5, op=ALU.is_gt), [trig_b, rp_b], [trig_b, rp_b])
                K.op("dve", lambda: V.tensor_tensor(out=tg1[:], in0=tg1[:], in1=tg2[:], op=ALU.subtract), [trig_b, rp_b], [trig_b, rp_b])
                K.op("dve", lambda: V.tensor_single_scalar(out=tg2[:], in_=tg1[:], scalar=-0.5, op=ALU.is_lt), [trig_b, rp_b], [trig_b, rp_b])
                K.op("dve", lambda: V.tensor_tensor(out=tg1[:], in0=tg1[:], in1=tg2[:], op=ALU.add), [trig_b, rp_b], [trig_b, rp_b])
                K.op("act", lambda dst=dst: A.activation(out=dst[:], in_=tg1[:], func=AF.Sin, scale=float(2 * np.pi)),
                     [trig_b, rp_b], [trig_b, rp_b])

            def comp_k(idx, st, sbf):
                for g in range(4):
                    for (c0, n) in [(a, min(512, WIN - a)) for a in range(0, WIN, 512)]:
                        pt, pb = next_ps()
                        mm_group(pt[:, 0:n], [(st[:, k, g * 128:(g + 1) * 128], h[:, k, c0:c0 + n]) for k in range(NCH)],
                                 [sbf, h_b], [pb])
                        fin = rope(pt, pb, kT[:, g, c0:c0 + n], c0, n)
                        fin([kT_b])
            run_slabs([(w_in, 0, 6144)], comp_k)

            def comp_v(idx, st, sbf):
                for blk in range(NBLK):
                    pt, pb = next_ps()
                    mm_group(pt[:, :], [(h[:, k, blk * 128:(blk + 1) * 128], st[:, k, :]) for k in range(NCH)], [sbf, h_b], [pb])
                    K.op("act", lambda blk=blk, pt=pt: A.copy(out=vtok[:, blk, :], in_=pt[:, :]), [pb], [v_b])
            run_slabs([(w_in, 0, 6656)], comp_v)

            def comp_q(g, st, sbf):
                qt, qbb = qb_t[g % 2], qb_b[g % 2]
                for hh in range(4):
                    pt, pb = next_ps()
                    mm_group(pt[:, 0:T], [(st[:, k, hh * 128:(hh + 1) * 128], h[:, k, 128:128 + T]) for k in range(NCH)], [sbf, h_b], [pb])
                    fin = rope(pt, pb, qt[:, hh, :], 128, T)
                    fin([qbb])
                for qb in range(TB):
                    gblk = j * TB + qb
                    pts = []
                    for kk in range(3):
                        kb = qb + kk
                        pt, pb = next_ps()
                        mm_group(pt[:, :], [(kT[:, g, kb * 128:(kb + 1) * 128], qt[:, :, qb * 128:(qb + 1) * 128])], [kT_b, qbb], [pb])
                        pi = (qb * 3 + kk) % 6
                        K.op("act", lambda pt=pt, pi=pi: A.activation(out=pT[pi][:], in_=pt[:, :], func=AF.Exp, scale=float(128 ** -0.5)),
                             [pb], [pT_b[pi]])
                        if kk != 1:
                            mi = (0 if kk == 0 else 1)
                            if kk == 0 and gblk == 0:
                                mi = 2
                            if kk == 2 and gblk == NT * TB - 1:
                                mi = 3
                            K.op("dve", lambda pi=pi, mi=mi: V.tensor_tensor(
                                out=pT[pi][:], in0=pT[pi][:], in1=masks_b[:, mi].rearrange("p a b -> p (a b)"), op=ALU.mult),
                                [pT_b[pi], cb], [pT_b[pi]])
                        pts.append(pi)
                    po, po_b = next_ps()
                    mm_group(po[:, :], [(vtok[:, qb + kk, g * 128:(g + 1) * 128], pT[pts[kk]][:]) for kk in range(3)],
                             [v_b] + [pT_b[p] for p in pts], [po_b])
                    pd, pd_b = next_ps()
                    mm_group(pd[:, :], [(ones_b[:], pT[pts[kk]][:]) for kk in range(3)], [cb] + [pT_b[p] for p in pts], [pd_b])
                    for hh in range(4):
                        K.op("dve", lambda hh=hh, pd=pd: V.tensor_scalar(
                            out=rden[:, hh * 128:(hh + 1) * 128], in0=pd[:, hh * 128:(hh + 1) * 128],
                            scalar1=esink[:, g * 4 + hh:g * 4 + hh + 1], scalar2=None, op0=ALU.add), [pd_b, cb], [rden_b])
                    K.op("dve", lambda: V.reciprocal(out=rden[:], in_=rden[:]), [rden_b], [rden_b])
                    K.op("dve", lambda po=po, qb=qb: V.tensor_tensor(
                        out=regA[:, 16 + g * 4:16 + g * 4 + 4, qb * 128:(qb + 1) * 128],
                        in0=po[:].rearrange("p (a b) -> p a b", a=4), in1=rden[:].rearrange("p (a b) -> p a b", a=4), op=ALU.mult),
                        [po_b, rden_b], [rA_b[16 + g * 4 + hh] for hh in range(4)])
            run_slabs([(w_in, 0, 4096 + g * 512) for g in range(4)], comp_q)

            def comp_qm(hx, st, sbf):
                qt, qbb = qb_t[hx % 2], qb_b[hx % 2]
                for dc in range(4):
                    pt, pb = next_ps()
                    mm_group(pt[:, 0:T], [(st[:, k, dc * 128:(dc + 1) * 128], h[:, k, 128:128 + T]) for k in range(NCH)], [sbf, h_b], [pb])
                    K.op("act", lambda pt=pt, dc=dc: A.copy(out=qt[:, dc, :], in_=pt[:, 0:T]), [pb], [qbb])
                pis = []
                for mb in range(2):
                    pt, pb = next_ps()
                    mm_group(pt[:, 0:T], [(mkT[:, hx * 4 + dc, mb * 128:(mb + 1) * 128], qt[:, dc, :]) for dc in range(4)],
                             [mk_b, qbb], [pb])
                    pi = (hx * 2 + mb) % 6
                    K.op("act", lambda pt=pt, pi=pi: A.activation(out=pT[pi][:, 0:T], in_=pt[:, 0:T], func=AF.Exp, scale=float(512 ** -0.5)),
                         [pb], [pT_b[pi]])
                    pis.append(pi)
                pd, pd_b = next_ps()
                mm_group(pd[:, 0:T], [(ones_b[:], pT[p][:, 0:T]) for p in pis], [cb] + [pT_b[p] for p in pis], [pd_b])
                K.op("dve", lambda pd=pd: V.reciprocal(out=rden[:, 0:T], in_=pd[:, 0:T]), [pd_b], [rden_b])
                for oc in range(4):
                    c = hx * 4 + oc
                    po, po_b = next_ps()
                    mm_group(po[:, 0:T], [(mv[:, mb, c * 128:(c + 1) * 128], pT[pis[mb]][:, 0:T]) for mb in range(2)],
                             [mv_b] + [pT_b[p] for p in pis], [po_b])
                    K.op("dve", lambda po=po, c=c: V.tensor_tensor(out=regA[:, 32 + c, :], in0=po[:, 0:T], in1=rden[:, 0:T], op=ALU.mult),
                         [po_b, rden_b], [rA_b[32 + c]])
            run_slabs([(w_in, 0, 7168 + g * 512) for g in range(4)], comp_qm)

            def comp_gr(og, st, sbf):
                for o4 in range(4):
                    c = og * 4 + o4
                    li = c % 2
                    lb, lbb = lbuf[li], lbuf_b[li]
                    K.dma("sp", lsem[li], lb[0][:], hf_s[j, :, c, :], reads=[hf_b[j][og]], writes=[lbb[0]])
                    K.dma("sp", lsem[li], lb[1][:], ab_s[j, :, c, :], reads=[ab_b[j][og]], writes=[lbb[1]])
                    K.dma("sp", lsem[li], lb[2][:], ub_s[j, :, c, :], reads=[ub_b[j][og]], writes=[lbb[2]])
                    pt, pb = next_ps()
                    mm_group(pt[:, 0:T], [(st[:, k, o4 * 128:(o4 + 1) * 128], h[:, k, 128:128 + T]) for k in range(NCH)], [sbf, h_b], [pb])
                    K.op("act", lambda pt=pt: A.activation(out=gel[:], in_=pt[:, 0:T], func=AF.Gelu), [pb], [gel_b])
                    K.op("dve", lambda lb=lb, c=c: V.tensor_tensor_scan(
                        out=hb[:, ::-1], data0=lb[1][:, ::-1], data1=lb[2][:, ::-1], initial=stb[:, c:c + 1],
                        op0=ALU.mult, op1=ALU.add), [lbb[1], lbb[2], stb_b], [hb_b])
                    K.op("dve", lambda c=c: V.tensor_copy(out=stb[:, c:c + 1], in_=hb[:, 0:1]), [hb_b], [stb_b])
                    K.op("pool", lambda lb=lb: G.tensor_tensor(out=hb[:], in0=hb[:], in1=lb[0][:], op=ALU.add), [hb_b, lbb[0]], [hb_b])
                    K.op("dve", lambda c=c: V.tensor_tensor(out=regA[:, c, :], in0=hb[:], in1=gel[:], op=ALU.mult),
                         [hb_b, gel_b], [rA_b[c]])
            run_slabs([(w_in, 0, 2048 + g * 512) for g in range(4)], comp_gr)

            for cbk in range(4):
                for b in range(3):
                    zt_, zb_ = load_slab(w_br[b], 0, cbk * 512)
                    g_t, g_b = load_slab(w_in, 0, 9216 + b * D + cbk * 512)
                    for o4 in range(4):
                        c = cbk * 4 + o4
                        pz, pz_b = next_ps()
                        mm_group(pz[:, 0:T], [(zt_[:, k, o4 * 128:(o4 + 1) * 128], regA[:, b * 16 + k, :]) for k in range(NCH)],
                                 [zb_] + [rA_b[b * 16 + k] for k in range(NCH)], [pz_b])
                        pg, pg_b = next_ps()
                        mm_group(pg[:, 0:T], [(g_t[:, k, o4 * 128:(o4 + 1) * 128], h[:, k, 128:128 + T]) for k in range(NCH)], [g_b, h_b], [pg_b])
                        gi = (o4 + b) % 2
                        K.op("act", lambda pg=pg, b=b, c=c, gi=gi: A.activation(
                            out=gtm[gi][:], in_=pg[:, 0:T], func=AF.Sigmoid, bias=bgate[:, b * 16 + c:b * 16 + c + 1]),
                            [pg_b, cb], [gtm_b[gi]])
                        if b == 0:
                            K.op("dve", lambda pz=pz, gi=gi, o4=o4: V.tensor_tensor(out=macc[o4][:], in0=gtm[gi][:], in1=pz[:, 0:T], op=ALU.mult),
                                 [gtm_b[gi], pz_b], [macc_b[o4]])
                        else:
                            K.op("dve", lambda pz=pz, gi=gi: V.tensor_tensor(out=gtm[gi][:], in0=gtm[gi][:], in1=pz[:, 0:T], op=ALU.mult),
                                 [gtm_b[gi], pz_b], [gtm_b[gi]])
                            if b == 1:
                                K.op("pool", lambda gi=gi, o4=o4: G.tensor_tensor(out=macc[o4][:], in0=macc[o4][:], in1=gtm[gi][:], op=ALU.add),
                                     [macc_b[o4], gtm_b[gi]], [macc_b[o4]])
                            else:
                                K.op("dve", lambda gi=gi, o4=o4, c=c: V.tensor_tensor(out=regA[:, 48 + c, :], in0=macc[o4][:], in1=gtm[gi][:], op=ALU.add),
                                     [macc_b[o4], gtm_b[gi]], [rA_b[48 + c]])

            def comp_out(cbk, st, sbf):
                for tb in range(TB):
                    pt, pb = next_ps()
                    mm_group(pt[:, :], [(regA[:, 48 + k, tb * 128:(tb + 1) * 128], st[:, k, :]) for k in range(NCH)],
                             [sbf] + [rA_b[48 + k] for k in range(NCH)], [pb])
                    mi = (cbk * TB + tb) % 2
                    K.op("act", lambda pt=pt, mi=mi: A.copy(out=mo_t[mi][:], in_=pt[:, :]), [pb], [mo_tb[mi]])
                    K.dma("sp", mosem[mi], mo_s[t0 + tb * 128:t0 + (tb + 1) * 128, cbk * 512:(cbk + 1) * 512], mo_t[mi][:],
                          reads=[mo_tb[mi]], writes=[mo_b[j][tb]])
            run_slabs([(w_out, 0, g * 512) for g in range(4)], comp_out)
            if DBG and j == 0:
                K.barrier()
                K.dma("sp", mosem[0], dbg_y, regA[:], reads=rA_b, writes=[])
                K.dma("sp", mosem[0], dbg_h, h[:], reads=[h_b], writes=[])
                K.barrier()
        K.barrier()

    mid.close()
    TF = 512 if L % 512 == 0 else T
    TBF = TF // 128
    NTF = L // TF
    mo_fb = [mo_b[j][t] for j in range(NT) for t in range(TB)]
    x1_fb = [x1_b[j][t] for j in range(NT) for t in range(TB)]
    with ExitStack() as es:
        gam = [sbt(es, "gam%d" % i, [128, D]) for i in range(3)]
        gam_b = Buf("gam")
        gsem = K.sem("gsem")
        for i in range(3):
            K.dma("sp", gsem, gam[i][:], rows_d[i, :].partition_broadcast(128), writes=[gam_b])
        hm = sbt(es, "hm", [128, NCH, TF], BF16)
        hm_b = Buf("hm")
        u = sbt(es, "u", [128, 64, TF], BF16)
        u_b = [Buf("u%d" % i) for i in range(64)]
        dd = sbt(es, "dd", [128, TBF, D])
        dd_b = [Buf("dd%d" % i) for i in range(TBF)]
        x1sem = K.sem("x1sem")
        rl = [sbt(es, "rl%d" % i, [128, TF]) for i in range(2)]
        rl_b = [Buf("rl%d" % i) for i in range(2)]
        osem = K.sem("osem")

        for j in range(NTF):
            t0 = j * TF
            for tb in range(TBF):
                r0 = t0 + tb * 128
                blk = r0 // 128
                K.dma("sp", tok_s[0], tok_t[0][:], mo_s[r0:r0 + 128, :], reads=[mo_fb[blk]], writes=[tok_b[0]])
                K.dma("sp", tok_s[1], tok_t[1][:], xw[128 + r0:128 + r0 + 128, :], writes=[tok_b[1]])
                rstd_from(tok_t[0], tok_b[0], xs_t[0], xs_b[0])
                K.op("dve", lambda: V.scalar_tensor_tensor(out=tok_t[0][:], in0=tok_t[0][:], scalar=st_t[:, 2:3], in1=gam[0][:],
                                                           op0=ALU.mult, op1=ALU.mult), [tok_b[0], st_b, gam_b], [tok_b[0]])
                K.op("pool", lambda: G.tensor_tensor(out=tok_t[1][:], in0=tok_t[0][:], in1=tok_t[1][:], op=ALU.add),
                     [tok_b[0], tok_b[1]], [tok_b[1]])
                K.dma("sp", x1sem, x1_s[r0:r0 + 128, :], tok_t[1][:], reads=[tok_b[1]], writes=[x1_fb[blk]])
                rstd_from(tok_t[1], tok_b[1], xs_t[0], xs_b[0])
                K.op("dve", lambda: V.scalar_tensor_tensor(out=xs_t[1][:], in0=tok_t[1][:], scalar=st_t[:, 2:3], in1=gam[1][:],
                                                           op0=ALU.mult, op1=ALU.mult), [tok_b[1], st_b, gam_b], [xs_b[1]])
                to_featmajor(xs_t[1], xs_b[1], hm, hm_b, tb * 128, None)

            def comp_up(idx, st, sbf):
                for o4 in range(4):
                    c = idx * 4 + o4
                    pt, pb = next_ps()
                    mm_group(pt[:, 0:TF], [(st[:, k, o4 * 128:(o4 + 1) * 128], hm[:, k, :]) for k in range(NCH)], [sbf, hm_b], [pb])
                    ri = c % 2
                    K.op("act", lambda pt=pt, ri=ri: A.activation(out=rl[ri][:], in_=pt[:, 0:TF], func=AF.Relu), [pb], [rl_b[ri]])
                    K.op("pool", lambda ri=ri, c=c: G.tensor_tensor(out=u[:, c, :], in0=rl[ri][:], in1=rl[ri][:], op=ALU.mult),
                         [rl_b[ri]], [u_b[c]])
            run_slabs([(w_up, 0, g * 512) for g in range(16)], comp_up)

            acc_ps = {}

            def comp_down(idx, st, sbf):
                cbk, kq = idx // 4, idx % 4
                for tb in range(TBF):
                    if kq == 0:
                        acc_ps[tb] = next_ps()
                    pt, pb = acc_ps[tb]

                    def fn(pt=pt, tb=tb, kq=kq):
                        ins = None
                        for k in range(NCH):
                            ins = nc.tensor.matmul(pt[:, :], lhsT=u[:, kq * 16 + k, tb * 128:(tb + 1) * 128], rhs=st[:, k, :],
                                                   start=(kq == 0 and k == 0), stop=(kq == 3 and k == NCH - 1))
                        return ins
                    K.op("pe", fn, [sbf] + [u_b[kq * 16 + k] for k in range(NCH)], [pb])
                    if kq == 3:
                        K.op("act", lambda pt=pt, tb=tb, cbk=cbk: A.copy(out=dd[:, tb, cbk * 512:(cbk + 1) * 512], in_=pt[:, :]),
                             [pb], [dd_b[tb]])
            run_slabs([(w_down, kq * D, cbk * 512) for cbk in range(4) for kq in range(4)], comp_down)

            for tb in range(TBF):
                r0 = t0 + tb * 128
                blk = r0 // 128
                K.dma("sp", tok_s[1], tok_t[1][:], x1_s[r0:r0 + 128, :], reads=[x1_fb[blk]], writes=[tok_b[1]])
                rstd_from(dd[:, tb, :], dd_b[tb], xs_t[0], xs_b[0])
                K.op("dve", lambda tb=tb: V.scalar_tensor_tensor(out=tok_t[0][:], in0=dd[:, tb, :], scalar=st_t[:, 2:3], in1=gam[2][:],
                                                                 op0=ALU.mult, op1=ALU.mult), [dd_b[tb], st_b, gam_b], [tok_b[0]])
                K.op("pool", lambda: G.tensor_tensor(out=tok_t[0][:], in0=tok_t[0][:], in1=tok_t[1][:], op=ALU.add),
                     [tok_b[0], tok_b[1]], [tok_b[0]])
                K.dma("sp", osem, out_d[r0:r0 + 128, :], tok_t[0][:], reads=[tok_b[0]], writes=[])
        K.barrier()
        if DBG:
            K.dma("sp", osem, dbg_x1, x1_s, reads=[], writes=[])
            K.dma("sp", osem, dbg_mo, mo_s, reads=[], writes=[])
            K.dma("sp", osem, dbg_hf, hf_s, reads=[], writes=[])
            K.barrier()
    top.close()
    return nc


_NC_CACHE = {}


def _chan(v):
    v = np.asarray(v, np.float32)
    lead = v.shape[:-1]
    r = v.reshape(lead + (16, 128))
    return np.ascontiguousarray(np.moveaxis(r, -1, 0))


def kernel(x, mem, positions, norm_mix_pre, norm_mix_post, norm_mem, w_in, b_gate, conv_w, conv_b,
           wr_f, br_f, wi_f, bi_f, lam_f, wr_b, br_b, wi_b, bi_b, lam_b, attn_sink, w_mem_kv,
           w_br_lru, w_br_attn, w_br_mem, w_out, norm_mlp_pre, norm_mlp_post, w_up, w_down):
    x = np.asarray(x, np.float32)
    B, S, _ = x.shape
    L = S // 4
    NT = L // T
    LW = L + 256
    if NT not in _NC_CACHE:
        _NC_CACHE[NT] = build(NT)
    nc = _NC_CACHE[NT]
    f = lambda a: np.ascontiguousarray(np.asarray(a, np.float32))
    xpad = np.zeros((B, S + 256, D), np.float32)
    xpad[:, 128:128 + S] = x
    pos = np.asarray(positions, np.int32)
    pospad = np.zeros((B, S + 256), np.int32)
    pospad[:, 128:128 + S] = pos
    cw = f(conv_w)[0]
    z = np.zeros_like(cw[0])
    tap_f = np.stack([z, cw[0], cw[1], cw[2], cw[3]], -1)
    tap_r = np.stack([cw[3], cw[2], cw[1], cw[0], z], -1)
    dirp = {
        "f": (f(wr_f)[0], f(wi_f)[0], f(br_f)[0].reshape(-1), f(bi_f)[0].reshape(-1), f(lam_f)[0]),
        "b": (f(wr_b)[0], f(wi_b)[0], f(br_b)[0].reshape(-1), f(bi_b)[0].reshape(-1), f(lam_b)[0]),
    }
    iu = np.arange(128)
    maskP = (iu[:, None] >= iu[None, :]).astype(np.float32)
    maskN = (iu[:, None] <= iu[None, :]).astype(np.float32)
    zero = np.zeros_like(maskP)
    half = 64
    fr = (10000.0 ** (-(np.arange(half, dtype=np.float32)) / half)).astype(np.float32)
    freq = np.stack([np.concatenate([fr, fr]), np.concatenate([fr, -fr])], -1).astype(np.float32)
    common = {
        "w_in": f(w_in)[0], "w_mem_kv": f(w_mem_kv)[0], "w_br_lru": f(w_br_lru)[0], "w_br_attn": f(w_br_attn)[0],
        "w_br_mem": f(w_br_mem)[0], "w_out": f(w_out)[0], "w_up": f(w_up)[0], "w_down": f(w_down)[0],
        "cbias": _chan(f(conv_b)[0]), "nmp": _chan(f(norm_mix_pre)[0]), "nmem": _chan(f(norm_mem)[0]),
        "bgate": np.ascontiguousarray(f(b_gate)[0].reshape(48, 128).T),
        "rows": np.ascontiguousarray(np.stack([f(norm_mix_post)[0], f(norm_mlp_pre)[0], f(norm_mlp_post)[0]], 0)),
        "sinkb": np.ascontiguousarray(np.broadcast_to(f(attn_sink)[0][None, :], (128, 16))),
        "freq": freq, "ident": np.eye(128, dtype=np.float32),
    }
    in_maps = []
    for c in range(8):
        b, q = c // 4, c % 4
        others = [("f", qq) for qq in range(q)] + [("b", qq) for qq in range(3, q, -1)]
        slots = [o[0] for o in others] + ["f", "b"]
        gw = np.stack([np.stack([np.transpose(dirp[s][0], (1, 0, 2)), np.transpose(dirp[s][1], (1, 0, 2))], 0) for s in slots], 0)
        gv = np.stack([np.stack([_chan(dirp[s][2]), _chan(dirp[s][3]), _chan(dirp[s][4])], 1) for s in slots], 1)
        taps = [tap_f if o[0] == "f" else tap_r for o in others] + [tap_f]
        ctap = np.stack([np.moveaxis(t_.reshape(16, 128, 5), 1, 0) for t_ in taps], 1)
        sel = np.zeros((128, 3, 2), np.float32)
        xo = np.zeros((3, LW, D), np.float32)
        for si, (dr, qq) in enumerate(others):
            win = xpad[b, qq * L:qq * L + LW]
            xo[si] = win if dr == "f" else win[::-1]
            sel[:, si, 0 if dr == "f" else 1] = 1.0
        first = (q == 0)
        last = (q == 3)
        masks = np.stack([maskP, maskN, zero if first else maskP, zero if last else maskN], 0)
        masks = np.ascontiguousarray(np.broadcast_to(masks[:, :, None, :], (4, 128, 4, 128)).transpose(1, 0, 2, 3))
        m = dict(common)
        m.update({
            "xw": np.ascontiguousarray(xpad[b, q * L:q * L + LW]), "xo": xo,
            "posw": np.ascontiguousarray(pospad[b:b + 1, q * L:q * L + LW]),
            "memb": f(mem)[b], "gw": np.ascontiguousarray(gw.astype(np.float32)), "gv": np.ascontiguousarray(gv),
            "ctap": np.ascontiguousarray(ctap), "sel": sel, "masks": masks,
        })
        in_maps.append(m)
    res = run_bass_kernel_spmd(nc, in_maps, core_ids=list(range(8)))
    global LAST_RES
    LAST_RES = res
    out = np.zeros((B, S, D), np.float32)
    for c in range(8):
        b, q = c // 4, c % 4
        out[b, q * L:(q + 1) * L] = res.results[c]["out"]
    return out
```
